# Optimizing a Trainium2 kernel written in Bass

```python
import jax, jax.numpy as jnp
from jax import lax
import numpy as np

D_MODEL = 1024
BATCH = 8
SEQ = 2048
DEPTH = 1

GRID_W = 64
HEAD_DIM = 64
D_MIX = D_MODEL
A_HEADS = 8
A_KV_HEADS = 2
B_HEADS = 8
A_WIDTH = A_HEADS * HEAD_DIM
A_KV_WIDTH = A_KV_HEADS * HEAD_DIM
B_WIDTH = B_HEADS * HEAD_DIM
SPLITS = (A_WIDTH, A_KV_WIDTH, A_KV_WIDTH, A_WIDTH, B_WIDTH, B_WIDTH, B_WIDTH, B_WIDTH)
D_IN = sum(SPLITS)
Q_BLOCK = 128
AXIS_DIM = HEAD_DIM // 2
ROPE_PAIRS = AXIS_DIM // 2
ROPE_THETA = 10000.0
NA_KH = 8
NA_KW = 16
NA_COL_BLOCK = 16
NA_BAND = NA_COL_BLOCK + NA_KW
EPS = 1e-6

kernel_name = "hymba_gqa_axialrope_natten_encoder"


def _rmsnorm(x, gain):
    x32 = x.astype(jnp.float32)
    y = x32 * lax.rsqrt(jnp.mean(x32 * x32, axis=-1, keepdims=True) + EPS)
    return (y * gain.astype(jnp.float32)).astype(x.dtype)


def _axial_rope_tables(seq, dtype):
    t = jnp.arange(seq, dtype=jnp.int32)
    row = (t // GRID_W).astype(jnp.float32)
    col = (t % GRID_W).astype(jnp.float32)
    inv = ROPE_THETA ** (-jnp.arange(ROPE_PAIRS, dtype=jnp.float32) * (2.0 / AXIS_DIM))
    ang_r = row[:, None] * inv[None, :]
    ang_c = col[:, None] * inv[None, :]
    ang = jnp.concatenate([ang_r, ang_r, ang_c, ang_c], axis=-1)
    return jnp.cos(ang)[:, None, :].astype(dtype), jnp.sin(ang)[:, None, :].astype(dtype)


def _rotate_half_axial(x):
    shp = x.shape
    xr = x.reshape(shp[:-1] + (2, 2, ROPE_PAIRS))
    x1 = xr[..., 0, :]
    x2 = xr[..., 1, :]
    return jnp.stack([-x2, x1], axis=-2).reshape(shp)


def _gqa_blocked(q, k, v):
    b, s, h, d = q.shape
    hk = k.shape[2]
    g = h // hk
    nb = s // Q_BLOCK
    qb = q.reshape(b, nb, Q_BLOCK, hk, g, d).transpose(1, 0, 2, 3, 4, 5)
    scale = d ** -0.5

    def block(qi):
        sc = jnp.einsum('bqkgd,bskd->bkgqs', qi, k).astype(jnp.float32) * scale
        p = jax.nn.softmax(sc, axis=-1).astype(v.dtype)
        return jnp.einsum('bkgqs,bskd->bqkgd', p, v)

    o = lax.map(block, qb)
    return o.transpose(1, 0, 2, 3, 4, 5).reshape(b, s, h * d)


def _neighbourhood_attn(q, k, v, rpb):
    b, s, h, d = q.shape
    rows = s // GRID_W
    kh = min(NA_KH, rows)
    n_cb = GRID_W // NA_COL_BLOCK
    qg = q.reshape(b, rows, GRID_W, h, d)
    kg = k.reshape(b, rows, GRID_W, h, d)
    vg = v.reshape(b, rows, GRID_W, h, d)
    c = np.arange(GRID_W)
    col_start = np.clip(c - NA_KW // 2, 0, GRID_W - NA_KW)
    band_start = np.clip(np.arange(n_cb) * NA_COL_BLOCK - NA_KW // 2, 0, GRID_W - NA_BAND)
    band_cols = band_start[:, None] + np.arange(NA_BAND)[None, :]
    qc = c.reshape(n_cb, NA_COL_BLOCK)
    qcs = col_start.reshape(n_cb, NA_COL_BLOCK)
    kc = band_cols[:, None, :]
    col_valid = (kc >= qcs[..., None]) & (kc < qcs[..., None] + NA_KW)
    col_idx = np.clip(kc - qc[..., None] + (NA_KW - 1), 0, 2 * NA_KW - 2)
    rpb_col = rpb[:, :, col_idx]
    mask = jnp.asarray(col_valid[:, :, None, :])
    scale = d ** -0.5

    def row_block(args):
        r, q_row = args
        r0 = jnp.clip(r - kh // 2, 0, rows - kh)
        k_rows = lax.dynamic_slice_in_dim(kg, r0, kh, axis=1)
        v_rows = lax.dynamic_slice_in_dim(vg, r0, kh, axis=1)
        k_band = jnp.take(k_rows, band_cols, axis=2)
        v_band = jnp.take(v_rows, band_cols, axis=2)
        qb = q_row.reshape(b, n_cb, NA_COL_BLOCK, h, d)
        sc = jnp.einsum('bjqhd,bkjchd->bhjqkc', qb, k_band).astype(jnp.float32) * scale
        row_idx = r0 + jnp.arange(kh) - r + (NA_KH - 1)
        bias = rpb_col[:, row_idx].transpose(0, 2, 3, 1, 4)
        sc = jnp.where(mask, sc + bias.astype(jnp.float32), -jnp.inf)
        p = jax.nn.softmax(sc, axis=(-2, -1)).astype(v.dtype)
        o = jnp.einsum('bhjqkc,bkjchd->bjqhd', p, v_band)
        return o.reshape(b, GRID_W, h * d)

    o = lax.map(row_block, (jnp.arange(rows), qg.transpose(1, 0, 2, 3, 4)))
    return o.transpose(1, 0, 2, 3).reshape(b, s, h * d)


def setup_inputs(seed: int = 0) -> dict:
    key = jax.random.key(seed)
    ks = jax.random.split(key, 9)
    x = jax.random.normal(ks[0], (BATCH, SEQ, D_MODEL), jnp.float32)
    norm_gain = 1.0 + 0.02 * jax.random.normal(ks[1], (DEPTH, D_MODEL), jnp.float32)
    w_in = jax.random.normal(ks[2], (DEPTH, D_MODEL, D_IN), jnp.float32) * D_MODEL ** -0.5
    q_norm_a = 1.0 + 0.02 * jax.random.normal(ks[3], (DEPTH, HEAD_DIM), jnp.float32)
    k_norm_a = 1.0 + 0.02 * jax.random.normal(ks[4], (DEPTH, HEAD_DIM), jnp.float32)
    na_rpb = 0.02 * jax.random.normal(ks[5], (DEPTH, B_HEADS, 2 * NA_KH - 1, 2 * NA_KW - 1), jnp.float32)
    w_out = jax.random.normal(ks[6], (DEPTH, D_MIX, D_MODEL), jnp.float32) * D_MIX ** -0.5
    final_norm_gain = 1.0 + 0.02 * jax.random.normal(ks[7], (D_MODEL,), jnp.float32)
    return {"x": x, "norm_gain": norm_gain, "w_in": w_in, "q_norm_a": q_norm_a,
            "k_norm_a": k_norm_a, "na_rpb": na_rpb, "w_out": w_out,
            "final_norm_gain": final_norm_gain}


def reference(x, norm_gain, w_in, q_norm_a, k_norm_a, na_rpb, w_out, final_norm_gain):
    b, s, _ = x.shape
    cos, sin = _axial_rope_tables(s, x.dtype)
    split_at = [int(v) for v in np.cumsum(SPLITS)[:-1]]
    for l in range(DEPTH):
        h = _rmsnorm(x, norm_gain[l])
        proj = h @ w_in[l]
        q_a, k_a, v_a, g_a, q_b, k_b, v_b, g_b = jnp.split(proj, split_at, axis=-1)
        q_a = _rmsnorm(q_a.reshape(b, s, A_HEADS, HEAD_DIM), q_norm_a[l])
        k_a = _rmsnorm(k_a.reshape(b, s, A_KV_HEADS, HEAD_DIM), k_norm_a[l])
        q_a = q_a * cos + _rotate_half_axial(q_a) * sin
        k_a = k_a * cos + _rotate_half_axial(k_a) * sin
        v_a = v_a.reshape(b, s, A_KV_HEADS, HEAD_DIM)
        o_a = _gqa_blocked(q_a, k_a, v_a) * jax.nn.silu(g_a)
        o_b = _neighbourhood_attn(q_b.reshape(b, s, B_HEADS, HEAD_DIM),
                                  k_b.reshape(b, s, B_HEADS, HEAD_DIM),
                                  v_b.reshape(b, s, B_HEADS, HEAD_DIM), na_rpb[l])
        o_b = o_b * jax.nn.silu(g_b)
        mixed = jnp.concatenate([o_a, o_b], axis=-1)
        x = x + mixed @ w_out[l]
    return _rmsnorm(x, final_norm_gain)
```

```python
import numpy as np
from contextlib import ExitStack
import concourse.bass as bass
import concourse.mybir as mybir
from concourse.bass_utils import run_bass_kernel_spmd

F32 = mybir.dt.float32
BF16 = mybir.dt.bfloat16
ACTF = mybir.ActivationFunctionType
ALU = mybir.AluOpType
AX = mybir.AxisListType

S_LEN = 2048
D = 1024
NT = 16
GRID_W = 64
ROWS = 32
EPS = 1e-6
NEG = -30000.0
D_IN = 3328


class Buf:
    __slots__ = ("name", "w", "r")

    def __init__(self, name=""):
        self.name = name
        self.w = None
        self.r = []


class DmaSem:
    def __init__(self, sem):
        self.sem = sem
        self.count = 0


class Prog:
    ENGS = ("pe", "act", "dve", "pool", "sp")

    def __init__(self, nc, sems):
        self.nc = nc
        self.sem = sems
        self.cnt = {e: 0 for e in self.ENGS}
        self.waited = {e: {} for e in self.ENGS}
        self.eng = {"pe": nc.tensor, "act": nc.scalar, "dve": nc.vector, "pool": nc.gpsimd, "sp": nc.sync}

    def _emit_wait(self, eng, ticket):
        if ticket is None:
            return
        sem, val = ticket
        key = id(sem)
        w = self.waited[eng]
        if w.get(key, 0) >= val:
            return
        w[key] = val
        self.eng[eng].wait_ge(sem, val)

    def op(self, eng, fns, reads=(), writes=(), extra_waits=(), dma=None):
        if callable(fns):
            fns = [fns]
        for b in reads:
            self._emit_wait(eng, b.w)
        for b in writes:
            self._emit_wait(eng, b.w)
            for t in b.r:
                self._emit_wait(eng, t)
        for t in extra_waits:
            self._emit_wait(eng, t)
        if dma is not None:
            dma.count += 16
            sem = dma.sem
            ticket = (sem, dma.count)
            inc = 16
        else:
            self.cnt[eng] += 1
            sem = self.sem[eng]
            ticket = (sem, self.cnt[eng])
            inc = 1
        e = self.eng[eng]
        n = len(fns)
        for i, fn in enumerate(fns):
            ins = fn(e)
            if i == n - 1:
                ins.then_inc(sem, inc)
        for b in reads:
            b.r.append(ticket)
        for b in writes:
            b.w = ticket
            b.r = []
        return ticket

    def wait(self, eng, ticket):
        self._emit_wait(eng, ticket)

    def dma_group(self, eng, fns, buf, dma):
        for t in ([buf.w] if buf.w else []) + buf.r:
            self._emit_wait(eng, t)
        e = self.eng[eng]
        for fn in fns:
            dma.count += 16
            fn(e).then_inc(dma.sem, 16)
        buf.w = (dma.sem, dma.count)
        buf.r = []
        return buf.w

    def fence(self, engines=None, on=("pe", "act", "dve", "pool")):
        for e in (engines or self.ENGS):
            for c in on:
                if c != e and self.cnt[c] > 0:
                    self._emit_wait(e, (self.sem[c], self.cnt[c]))

    def flush(self):
        self.fence()


def pipeline(stages, n):
    mx = max(sk for _, sk in stages)
    for step in range(n + mx):
        for f, sk in stages:
            i = step - sk
            if 0 <= i < n:
                f(i)


def _r0(r):
    return min(max(r - 4, 0), ROWS - 8)


def build_nc():
    nc = bass.Bass("TRN2", target_bir_lowering=False)
    x_d = nc.dram_tensor("x", [S_LEN, D], F32, kind="ExternalInput").ap()
    win_d = nc.dram_tensor("w_in", [D, D_IN], F32, kind="ExternalInput").ap()
    wout_d = nc.dram_tensor("w_out", [D, D], F32, kind="ExternalInput").ap()
    ng_d = nc.dram_tensor("norm_gain", [D], F32, kind="ExternalInput").ap()
    fg_d = nc.dram_tensor("final_norm_gain", [D], F32, kind="ExternalInput").ap()
    qg_d = nc.dram_tensor("q_norm_a", [64], F32, kind="ExternalInput").ap()
    kg_d = nc.dram_tensor("k_norm_a", [64], F32, kind="ExternalInput").ap()
    cos_d = nc.dram_tensor("cos_tab", [128, NT * 64], F32, kind="ExternalInput").ap()
    sin_d = nc.dram_tensor("sin_tab", [128, NT * 64], F32, kind="ExternalInput").ap()
    nab_d = nc.dram_tensor("na_bias", [128, 8 * 15 * 64], F32, kind="ExternalInput").ap()
    nam_d = nc.dram_tensor("na_mask", [128, 64], F32, kind="ExternalInput").ap()
    y_d = nc.dram_tensor("y", [S_LEN, D], F32, kind="ExternalOutput").ap()

    win_v = win_d.rearrange("(kc p) n -> p kc n", p=128)
    wout_v = wout_d.rearrange("(kc p) n -> p kc n", p=128)

    with ExitStack() as L0:
        def sb(es, name, shape, dt):
            return es.enter_context(nc.sbuf_tensor(name, shape, dt))

        def ps(es, name, shape, dt):
            return es.enter_context(nc.psum_tensor(name, shape, dt))

        sems = {e: L0.enter_context(nc.semaphore("s_" + e)) for e in Prog.ENGS}
        P = Prog(nc, sems)

        def dsem(name):
            return DmaSem(L0.enter_context(nc.semaphore(name)))

        gT_a = sb(L0, "gT_a", [128, 4, S_LEN], BF16)
        gT_b = sb(L0, "gT_b", [128, 4, S_LEN], BF16)
        xT = sb(L0, "xT", [128, 8, S_LEN], BF16)
        W_b = sb(L0, "W_b", [128, 8, 2048], BF16)
        B_Wb = [Buf() for _ in range(4)]
        ident = sb(L0, "ident", [128, 128], BF16)
        idf = sb(L0, "idf", [128, 128], F32)
        B_gTa = [[Buf() for _ in range(4)] for _ in range(4)]
        B_gTb = [[Buf() for _ in range(4)] for _ in range(4)]
        B_xT = [Buf() for _ in range(NT)]
        B_ident = Buf()
        B_idf = Buf()

        P.op("pool", lambda e: e.memset(idf[:], 0.0), writes=[B_idf])
        P.op("pool", lambda e: e.affine_select(out=idf[:], in_=idf[:], compare_op=ALU.not_equal, fill=1.0,
                                               base=0, pattern=[[-1, 128]], channel_multiplier=1),
             reads=[B_idf], writes=[B_idf])
        P.op("dve", lambda e: e.tensor_copy(ident[:], idf[:]), reads=[B_idf], writes=[B_ident])

        with ExitStack() as LA:
            qkT_a = sb(LA, "qkT_a", [128, 8, S_LEN], BF16)
            v_a = sb(LA, "v_a", [128, NT, 2, 128], BF16)
            v_a2 = sb(LA, "v_a2", [128, NT, 2, 128], BF16)
            B_qkTa = [Buf() for _ in range(NT)]
            B_va = [Buf() for _ in range(NT)]
            B_vaones = Buf()
            P.op("pool", lambda e: e.memset(v_a[:, :, :, 64:128], 1.0), writes=[B_vaones])
            P.op("pool", lambda e: e.memset(v_a2[:, :, :, 0:64], 1.0), writes=[B_vaones])

            with ExitStack() as L2:
                W_a = sb(L2, "W_a", [128, 8, 1280], BF16)
                B_Wa = [Buf(), Buf(), Buf()]
                d_wa = [dsem("d_wa%d" % i) for i in range(3)]
                col_rng = [(0, 512), (512, 768), (768, 1280)]
                def load_wa(pi):
                    c0, c1 = col_rng[pi]
                    P.dma_group("pool", [lambda e, kc=kc: e.dma_start(out=W_a[:, kc, c0:c1], in_=win_v[:, kc, c0:c1])
                                         for kc in range(8)], B_Wa[pi], d_wa[pi])

                load_wa(0)
                load_wa(1)

                gain_bc = sb(L2, "gain_bc", [128, D], F32)
                qg_bc = sb(L2, "qg_bc", [128, 64], F32)
                kg_bc = sb(L2, "kg_bc", [128, 64], F32)
                cgq = sb(L2, "cgq", [128, NT, 64], F32)
                sgq = sb(L2, "sgq", [128, NT, 64], F32)
                cgk = sb(L2, "cgk", [128, NT, 64], F32)
                sgk = sb(L2, "sgk", [128, NT, 64], F32)
                B_gain, B_qg, B_kg = Buf(), Buf(), Buf()
                B_cgq, B_sgq, B_cgk, B_sgk = Buf(), Buf(), Buf(), Buf()
                d_c = [dsem("d_c%d" % i) for i in range(3)]
                P.op("sp", lambda e: e.dma_start(out=gain_bc[:], in_=ng_d.partition_broadcast(128)), writes=[B_gain], dma=d_c[0])
                P.op("sp", lambda e: e.dma_start(out=qg_bc[:], in_=qg_d.partition_broadcast(128)), writes=[B_qg], dma=d_c[1])
                P.op("sp", lambda e: e.dma_start(out=kg_bc[:], in_=kg_d.partition_broadcast(128)), writes=[B_kg], dma=d_c[2])

                with ExitStack() as L1:
                    xs = [sb(L1, "xs%d" % i, [128, D], F32) for i in range(4)]
                    xn = [sb(L1, "xn%d" % i, [128, D], BF16) for i in range(2)]
                    sqj = [sb(L1, "sqj%d" % i, [128, D], BF16) for i in range(1)]
                    stat = sb(L1, "stat", [128, NT, 4], F32)
                    tp = [ps(L1, "tp%d" % i, [128, 8, 128], BF16) for i in range(2)]
                    B_xs, B_xn, B_sqj, B_tp = [Buf(), Buf(), Buf(), Buf()], [Buf(), Buf()], [Buf(), Buf()], [Buf(), Buf()]
                    B_stat = [Buf() for _ in range(NT)]
                    B_statall = Buf()
                    d_x = [dsem("d_x%d" % i) for i in range(4)]
                    d_t = [dsem("d_t%d" % i) for i in range(4)]

                    def load_tabs():
                        for ti, (tl, src, Bt) in enumerate(((cgq, cos_d, B_cgq), (sgq, sin_d, B_sgq), (cgk, cos_d, B_cgk), (sgk, sin_d, B_sgk))):
                            P.op("act", lambda e, tl=tl, src=src: e.dma_start(out=tl[:].rearrange("p t d -> p (t d)"), in_=src[:, :]), writes=[Bt], dma=d_t[ti])
                    P.op("dve", lambda e: e.memset(stat[:], 0.0), writes=[B_statall])

                    def fold_tables(cg, sg, g_bc, B_cg, B_sg, B_g, scale):
                        g4 = g_bc[:].rearrange("p (c h e) -> p c h e", c=2, h=2)
                        P.op("dve", lambda e: e.tensor_tensor(out=cg[:], in0=cg[:], in1=g_bc[:, None, :].broadcast_to([128, NT, 64]), op=ALU.mult),
                             reads=[B_cg, B_g], writes=[B_cg])
                        sg5 = sg[:].rearrange("p t (c h e) -> p t c h e", c=2, h=2)
                        for hf in range(2):
                            P.op("dve", lambda e, hf=hf: e.tensor_tensor(
                                out=sg5[:, :, :, hf, :], in0=sg5[:, :, :, hf, :],
                                in1=g4[:, None, :, 1 - hf, :].broadcast_to([128, NT, 2, 16]), op=ALU.mult),
                                reads=[B_sg, B_g], writes=[B_sg])
                        if scale != 1.0:
                            P.op("dve", lambda e: e.tensor_scalar(cg[:], cg[:], scale, None, ALU.mult), reads=[B_cg], writes=[B_cg])
                            P.op("dve", lambda e: e.tensor_scalar(sg[:], sg[:], scale, None, ALU.mult), reads=[B_sg], writes=[B_sg])


                    def p1_load(t):
                        s3 = t % 4
                        tk = P.op("sp", lambda e: e.dma_start(out=xs[s3][:], in_=x_d[t * 128:(t + 1) * 128, :]), writes=[B_xs[s3]], dma=d_x[s3])
                        if t == 8:
                            P.wait("pool", tk)
                            load_wa(2)
                            load_tabs()

                    def p1_sq(t):
                        s3 = t % 4
                        P.op("act", lambda e: e.activation(out=sqj[0][:], in_=xs[s3][:], func=ACTF.Square, accum_out=stat[:, t, 0:1]),
                             reads=[B_xs[s3], B_statall], writes=[B_sqj[0], B_stat[t]])

                    def p1_scale(t):
                        P.op("dve", lambda e: e.tensor_scalar(stat[:, t, 1:2], stat[:, t, 0:1], 1.0 / D, EPS, ALU.mult, ALU.add),
                             reads=[B_stat[t]], writes=[B_stat[t]])

                    def p1_sqrt(t):
                        P.op("act", lambda e: e.activation(out=stat[:, t, 1:2], in_=stat[:, t, 1:2], func=ACTF.Sqrt), reads=[B_stat[t]], writes=[B_stat[t]])

                    def p1_norm(t):
                        s3, s = t % 4, t % 2
                        P.op("dve", lambda e: e.reciprocal(stat[:, t, 2:3], stat[:, t, 1:2]), reads=[B_stat[t]], writes=[B_stat[t]])
                        P.op("dve", lambda e: e.scalar_tensor_tensor(out=xn[s][:], in0=xs[s3][:], scalar=stat[:, t, 2:3], in1=gain_bc[:],
                                                                      op0=ALU.mult, op1=ALU.mult),
                             reads=[B_xs[s3], B_stat[t], B_gain], writes=[B_xn[s]])

                    def p1_tr(t):
                        s = t % 2
                        P.op("pe", [lambda e, kc=kc: e.transpose(tp[s][:, kc, :], xn[s][:, kc * 128:(kc + 1) * 128], ident[:]) for kc in range(8)],
                             reads=[B_xn[s], B_ident], writes=[B_tp[s]])

                    def p1_evac(t):
                        s = t % 2
                        if t % 2 == 0:
                            P.op("act", lambda e: e.copy(xT[:, :, t * 128:(t + 1) * 128], tp[s][:]), reads=[B_tp[s]], writes=[B_xT[t]])
                        else:
                            P.op("dve", lambda e: e.tensor_copy(xT[:, :, t * 128:(t + 1) * 128], tp[s][:]), reads=[B_tp[s]], writes=[B_xT[t]])

                    pipeline([(p1_load, 0), (p1_sq, 0), (p1_sqrt, 1), (p1_norm, 1), (p1_tr, 1), (p1_evac, 2), (p1_scale, 0)], NT)
                    P.flush()
                    fold_tables(cgq, sgq, qg_bc, B_cgq, B_sgq, B_qg, 0.125)
                    fold_tables(cgk, sgk, kg_bc, B_cgk, B_sgk, B_kg, 1.0)

                with ExitStack() as L2a:
                    qps = [ps(L2a, "qps%d" % i, [128, 512], F32) for i in range(2)]
                    kvps = [ps(L2a, "kvps%d" % i, [128, 512], F32) for i in range(2)]
                    tpq = [ps(L2a, "tpq%d" % i, [128, 8, 128], BF16) for i in range(2)]
                    gps = [ps(L2a, "gps%d" % i, [128, 512], F32) for i in range(2)]
                    sqf = [sb(L2a, "sqf%d" % i, [128, 640], F32) for i in range(2)]
                    rs = [sb(L2a, "rs%d" % i, [128, 10], F32) for i in range(2)]
                    t1 = [sb(L2a, "t1_%d" % i, [128, 640], F32) for i in range(2)]
                    t2 = [sb(L2a, "t2_%d" % i, [128, 640], F32) for i in range(2)]
                    qkr = [sb(L2a, "qkr%d" % i, [128, 1024], BF16) for i in range(2)]
                    B_qkz = [Buf(), Buf()]
                    for i_ in range(2):
                        P.op("pool", lambda e, i_=i_: e.memset(qkr[i_][:, 512:1024], 0.0), writes=[B_qkz[i_]])
                    B_qps, B_kvps, B_tpq, B_gps = [Buf(), Buf()], [Buf(), Buf()], [Buf(), Buf()], [Buf(), Buf()]
                    B_sqf, B_rs, B_qhat, B_t1, B_t2, B_qkr = ([Buf(), Buf()] for _ in range(6))

                    def kz_dst(tile_):
                        base = tile_[:, 512:1024]
                        return bass.AP(base.tensor, base.offset, [list(base.ap[0]), [256, 2], [192, 2], [1, 64]])

                    def a_mm(t):
                        s = t % 2
                        tok = slice(t * 128, (t + 1) * 128)
                        P.op("pe", [lambda e, kc=kc: e.matmul(qps[s][:], lhsT=xT[:, kc, tok], rhs=W_a[:, kc, 0:512],
                                                               start=(kc == 0), stop=(kc == 7)) for kc in range(8)],
                             reads=[B_xT[t], B_Wa[0]], writes=[B_qps[s]])
                        P.op("pe", [lambda e, kc=kc: e.matmul(kvps[s][:, 0:256], lhsT=xT[:, kc, tok], rhs=W_a[:, kc, 512:768],
                                                               start=(kc == 0), stop=(kc == 7)) for kc in range(8)],
                             reads=[B_xT[t], B_Wa[1]], writes=[B_kvps[s]])

                    sq_tk = {}

                    def a_sq(t):
                        s = t % 2
                        P.op("act", lambda e: e.activation(out=sqf[s][:, 0:512], in_=qps[s][:], func=ACTF.Square), reads=[B_qps[s]], writes=[B_sqf[s]])
                        sq_tk[t] = P.op("act", lambda e: e.activation(out=sqf[s][:, 512:640], in_=kvps[s][:, 0:128], func=ACTF.Square),
                                        reads=[B_kvps[s]], writes=[B_sqf[s]])

                    def a_rope(t):
                        s = t % 2
                        P.wait("dve", sq_tk[t])
                        for (lo, hi, nh, cg, sg, Bc, Bs, src, Bsrc) in ((0, 512, 8, cgq, sgq, B_cgq, B_sgq, qps[s][:, 0:512], B_qps[s]),
                                                                          (512, 640, 2, cgk, sgk, B_cgk, B_sgk, kvps[s][:, 0:128], B_kvps[s])):
                            P.op("dve", lambda e, lo=lo, hi=hi, nh=nh, cg=cg, src=src: e.tensor_tensor(
                                out=t1[s][:, lo:hi].rearrange("p (h d) -> p h d", d=64),
                                in0=src.rearrange("p (h d) -> p h d", d=64),
                                in1=cg[:, t, None, :].broadcast_to([128, nh, 64]), op=ALU.mult),
                                reads=[Bsrc, Bc], writes=[B_t1[s]])
                            for hf in range(2):
                                P.op("dve", lambda e, lo=lo, hi=hi, nh=nh, sg=sg, hf=hf, src=src: e.tensor_tensor(
                                    out=t2[s][:, lo:hi].rearrange("p (h c f e) -> p h c f e", c=2, f=2, e=16)[:, :, :, hf, :],
                                    in0=src.rearrange("p (h c f e) -> p h c f e", c=2, f=2, e=16)[:, :, :, 1 - hf, :],
                                    in1=sg[:, t, :].rearrange("p (c f e) -> p c f e", c=2, f=2)[:, None, :, hf, :].broadcast_to([128, nh, 2, 16]),
                                    op=ALU.mult),
                                    reads=[Bsrc, Bs], writes=[B_t2[s]])
                        P.op("dve", lambda e: e.tensor_copy(v_a[:, t, :, 0:64], kvps[s][:, 128:256].rearrange("p (h d) -> p h d", d=64)),
                             reads=[B_kvps[s]], writes=[B_va[t]])
                        P.op("dve", lambda e: e.tensor_copy(v_a2[:, t, :, 64:128], kvps[s][:, 128:256].rearrange("p (h d) -> p h d", d=64)),
                             reads=[B_kvps[s]], writes=[B_va[t]])

                    def a_red(t):
                        s = t % 2
                        P.op("dve", lambda e: e.tensor_reduce(out=rs[s][:], in_=sqf[s][:].rearrange("p (h d) -> p h d", d=64), axis=AX.X, op=ALU.add),
                             reads=[B_sqf[s]], writes=[B_rs[s]])
                        P.op("dve", lambda e: e.tensor_scalar(rs[s][:], rs[s][:], 1.0 / 64, EPS, ALU.mult, ALU.add), reads=[B_rs[s]], writes=[B_rs[s]])

                    def a_sqrt(t):
                        s = t % 2
                        P.op("act", lambda e: e.activation(out=rs[s][:], in_=rs[s][:], func=ACTF.Sqrt), reads=[B_rs[s]], writes=[B_rs[s]])

                    def a_rec(t):
                        s = t % 2
                        P.op("dve", lambda e: e.reciprocal(rs[s][:], rs[s][:]), reads=[B_rs[s]], writes=[B_rs[s]])

                    def a_comb(t):
                        s = t % 2
                        P.op("pool", lambda e: e.tensor_tensor(out=t1[s][:], in0=t1[s][:], in1=t2[s][:], op=ALU.add),
                             reads=[B_t1[s], B_t2[s]], writes=[B_t1[s]])
                        P.op("pool", lambda e: e.tensor_tensor(out=qkr[s][:, 0:512].rearrange("p (h d) -> p h d", d=64),
                                                               in0=t1[s][:, 0:512].rearrange("p (h d) -> p h d", d=64),
                                                               in1=rs[s][:, 0:8, None].broadcast_to([128, 8, 64]), op=ALU.mult),
                             reads=[B_t1[s], B_rs[s]], writes=[B_qkr[s]])
                        P.op("pool", lambda e: e.tensor_tensor(
                            out=kz_dst(qkr[s]),
                            in0=t1[s][:, 512:640].rearrange("p (k d) -> p k d", k=2)[:, :, None, :].broadcast_to([128, 2, 2, 64]),
                            in1=rs[s][:, 8:10, None, None].broadcast_to([128, 2, 2, 64]), op=ALU.mult),
                             reads=[B_t1[s], B_rs[s], B_qkz[s]], writes=[B_qkr[s]])

                    def a_tr(t):
                        s = t % 2
                        P.op("pe", [lambda e, i=i: e.transpose(tpq[s][:, i, :], qkr[s][:, i * 128:(i + 1) * 128], ident[:]) for i in range(8)],
                             reads=[B_qkr[s], B_ident], writes=[B_tpq[s]])

                    def a_ev(t):
                        s = t % 2
                        P.op("act", lambda e: e.copy(qkT_a[:, :, t * 128:(t + 1) * 128], tpq[s][:]), reads=[B_tpq[s]], writes=[B_qkTa[t]])

                    def gate_grp(g):
                        return divmod(g, 4)

                    def a_gate_mm(t):
                        if t % 2 == 0:
                            return
                        for g in (t - 1, t):
                            fb, tc = gate_grp(g)
                            s = g % 2
                            P.op("pe", [lambda e, s=s, kc=kc, fb=fb, tc=tc: e.matmul(gps[s][:], lhsT=W_a[:, kc, 768 + fb * 128:768 + (fb + 1) * 128],
                                                                                        rhs=xT[:, kc, tc * 512:(tc + 1) * 512], start=(kc == 0), stop=(kc == 7))
                                        for kc in range(8)],
                                 reads=[B_Wa[2]] + B_xT[tc * 4:(tc + 1) * 4], writes=[B_gps[s]])

                    def a_gate_act(t):
                        if t % 2 == 0:
                            return
                        for g in (t - 1, t):
                            fb, tc = gate_grp(g)
                            s = g % 2
                            P.op("act", lambda e, s=s, fb=fb, tc=tc: e.activation(out=gT_a[:, fb, tc * 512:(tc + 1) * 512], in_=gps[s][:], func=ACTF.Silu),
                                 reads=[B_gps[s]], writes=[B_gTa[fb][tc]])

                    pipeline([(a_mm, 0), (a_sqrt, 2), (a_sq, 0), (a_rope, 1), (a_red, 1), (a_rec, 2), (a_comb, 2), (a_tr, 3), (a_ev, 3),
                              (a_gate_act, 1), (a_gate_mm, 0)], NT)
                    P.flush()

            with ExitStack() as L3a:
                NS = 3
                Sps = [ps(L3a, "Sps%d" % i, [128, 2, 512], F32) for i in range(NS)]
                acc = ps(L3a, "acc", [128, 2, 512], F32)
                NPT = 3
                Pt = [sb(L3a, "Pt%d" % i, [128, 2, 512], BF16) for i in range(NPT)]
                accs = [sb(L3a, "accs%d" % i, [128, 2, 512], F32) for i in range(2)]
                rec = [sb(L3a, "rec%d" % i, [128, 512], F32) for i in range(2)]
                onm = [sb(L3a, "onm%d" % i, [128, 512], F32) for i in range(2)]
                B_S = [Buf() for _ in range(NS)]
                B_acc = Buf()
                B_accs, B_rec, B_onm = [Buf(), Buf()], [Buf(), Buf()], [Buf(), Buf()]
                B_Pt = [Buf() for _ in range(NPT)]

                d_wb = [dsem("d_wb%d" % i) for i in range(4)]
                for pi in range(4):
                    P.dma_group("pool", [lambda e, kc=kc, pi=pi: e.dma_start(out=W_b[:, kc, pi * 512:(pi + 1) * 512],
                                                                              in_=win_v[:, kc, 1280 + pi * 512:1280 + (pi + 1) * 512])
                                         for kc in range(8)], B_Wb[pi], d_wb[pi])

                steps = [(p, c, kt) for p in range(4) for c in range(4) for kt in range(NT)]

                def emit_S(i):
                    p, c, kt = steps[i]
                    kv = p // 2
                    sl = i % NS
                    keys = slice(kt * 128, (kt + 1) * 128)
                    qs = slice(c * 512, (c + 1) * 512)
                    P.op("pe", [lambda e: e.matmul(Sps[sl][:, 0, :], lhsT=qkT_a[:, 4 + 2 * kv, keys], rhs=qkT_a[:, p, qs], start=True, stop=True),
                                lambda e: e.matmul(Sps[sl][:, 1, :], lhsT=qkT_a[:, 5 + 2 * kv, keys], rhs=qkT_a[:, p, qs], start=True, stop=True)],
                         reads=[B_qkTa[kt]] + B_qkTa[c * 4:(c + 1) * 4], writes=[B_S[sl]])

                def emit_exp(i):
                    sl = i % NS
                    pl = i % NPT
                    P.op("act", lambda e: e.activation(out=Pt[pl][:], in_=Sps[sl][:], func=ACTF.Exp), reads=[B_S[sl]], writes=[B_Pt[pl]])

                def emit_PV(i):
                    p, c, kt = steps[i]
                    kv = p // 2
                    pl = i % NPT
                    al = (i // NT) % 2
                    P.op("pe", [lambda e: e.matmul(acc[:, 0, :], lhsT=v_a[:, kt, kv, :], rhs=Pt[pl][:, 0, :], start=(kt == 0), stop=(kt == NT - 1)),
                                lambda e: e.matmul(acc[:, 1, :], lhsT=v_a2[:, kt, kv, :], rhs=Pt[pl][:, 1, :], start=(kt == 0), stop=(kt == NT - 1))],
                         reads=[B_va[kt], B_vaones, B_Pt[pl]], writes=[B_acc])
                    if kt == NT - 1:
                        qs = slice(c * 512, (c + 1) * 512)
                        P.op("dve", lambda e: e.tensor_copy(accs[al][:], acc[:]), reads=[B_acc], writes=[B_accs[al]])
                        P.op("dve", lambda e: e.reciprocal(rec[al][0:64, :], accs[al][64:128, 0, :]), reads=[B_accs[al]], writes=[B_rec[al]])
                        P.op("dve", lambda e: e.reciprocal(rec[al][64:128, :], accs[al][0:64, 1, :]), reads=[B_accs[al]], writes=[B_rec[al]])
                        P.op("dve", lambda e: e.tensor_tensor(out=onm[al][0:64, :], in0=accs[al][0:64, 0, :], in1=rec[al][0:64, :], op=ALU.mult),
                             reads=[B_accs[al], B_rec[al]], writes=[B_onm[al]])
                        P.op("dve", lambda e: e.tensor_tensor(out=onm[al][64:128, :], in0=accs[al][64:128, 1, :], in1=rec[al][64:128, :], op=ALU.mult),
                             reads=[B_accs[al], B_rec[al]], writes=[B_onm[al]])
                        P.op("pool", lambda e: e.tensor_tensor(out=gT_a[:, p, qs], in0=onm[al][:], in1=gT_a[:, p, qs], op=ALU.mult),
                             reads=[B_onm[al], B_gTa[p][c]], writes=[B_gTa[p][c]])

                n = len(steps)
                emit_S(0)
                emit_S(1)
                for i in range(n):
                    if i + 2 < n:
                        emit_S(i + 2)
                    emit_exp(i)
                    emit_PV(i)
                P.fence(engines=("act", "dve", "pool", "sp"))

        with ExitStack() as LB:
            qT_b = sb(LB, "qT_b", [128, 4, S_LEN], BF16)
            kz_b = sb(LB, "kz_b", [128, 8, S_LEN], BF16)
            B_kz0 = Buf()
            kz4 = kz_b[:].rearrange("p (a b) s -> p a b s", b=2)
            P.op("pool", lambda e: e.memset(kz4[64:128, :, 0, :], 0.0), writes=[B_kz0])
            P.op("pool", lambda e: e.memset(kz4[0:64, :, 1, :], 0.0), writes=[B_kz0])
            v_b = sb(LB, "v_b", [128, NT, 8, 128], BF16)
            nam = sb(LB, "na_msk", [128, 64], F32)
            B_qkTb = [[Buf() for _ in range(4)] for _ in range(8)]
            B_vb = [Buf() for _ in range(NT)]
            B_vbones, B_nam = Buf(), Buf()
            d_nam = dsem("d_nam")
            P.op("pool", lambda e: e.memset(v_b[:, :, :, 64:128], 1.0), writes=[B_vbones])
            P.op("sp", lambda e: e.dma_start(out=nam[:], in_=nam_d[:, :]), writes=[B_nam], dma=d_nam)
            tabh = [sb(LB, "tabh%d" % i, [128, 15, 64], F32) for i in range(2)]
            tabb = [sb(LB, "tabb%d" % i, [128, 16, 64], BF16) for i in range(2)]
            tabm = [sb(LB, "tabm%d" % i, [128, 16, 64], BF16) for i in range(2)]
            B_tabh, B_tabb, B_tabm = [Buf(), Buf()], [Buf(), Buf()], [Buf(), Buf()]
            d_tab = [dsem("d_tab0"), dsem("d_tab1")]

            def load_tab(h):
                ts = h % 2
                P.op("sp", lambda e: e.dma_start(out=tabh[ts][:].rearrange("p b c -> p (b c)"), in_=nab_d[:, h * 960:(h + 1) * 960]),
                     writes=[B_tabh[ts]], dma=d_tab[ts])
                P.op("pool", lambda e: e.tensor_tensor(out=tabb[ts][:, 0:15, :], in0=tabh[ts][:], in1=nam[:, None, :].broadcast_to([128, 15, 64]), op=ALU.add),
                     reads=[B_tabh[ts], B_nam], writes=[B_tabb[ts]])
                P.op("pool", lambda e: e.tensor_copy(tabm[ts][:, 0:15, :], tabb[ts][:, 0:15, :]), reads=[B_tabb[ts]], writes=[B_tabm[ts]])
                P.op("pool", lambda e: e.memset(tabm[ts][64:128, 4, :], NEG), writes=[B_tabm[ts]])
                P.op("pool", lambda e: e.memset(tabm[ts][0:64, 12, :], NEG), writes=[B_tabm[ts]])

            for ts_ in range(2):
                P.op("pool", lambda e, ts_=ts_: e.memset(tabb[ts_][:, 15:16, :], 0.0), writes=[B_tabb[ts_]])
                P.op("pool", lambda e, ts_=ts_: e.memset(tabm[ts_][:, 15:16, :], 0.0), writes=[B_tabm[ts_]])
            load_tab(0)

            with ExitStack() as L2b:
                NFP = 6
                fps = [ps(L2b, "fps%d" % i, [128, 512], F32) for i in range(NFP)]
                vps = [ps(L2b, "vps%d" % i, [128, 512], F32) for i in range(2)]
                B_fps = [Buf() for _ in range(NFP)]
                B_vps = [Buf(), Buf()]
                fi = 0
                for grp in (0, 1, 3):
                    for fb in range(4):
                        for tc in range(4):
                            s = fi % NFP
                            fi += 1
                            col0 = grp * 512 + fb * 128
                            P.op("pe", [lambda e, s=s, kc=kc, col0=col0, tc=tc: e.matmul(fps[s][:], lhsT=W_b[:, kc, col0:col0 + 128],
                                                                                          rhs=xT[:, kc, tc * 512:(tc + 1) * 512], start=(kc == 0), stop=(kc == 7))
                                        for kc in range(8)],
                                 reads=[B_Wb[grp]] + B_xT[tc * 4:(tc + 1) * 4], writes=[B_fps[s]])
                            qs = slice(tc * 512, (tc + 1) * 512)
                            if grp == 0:
                                P.op("dve", lambda e, s=s, fb=fb, qs=qs: e.tensor_scalar(qT_b[:, fb, qs], fps[s][:], 0.125, None, ALU.mult),
                                     reads=[B_fps[s]], writes=[B_qkTb[fb][tc]])
                            elif grp == 1:
                                P.op("act", lambda e, s=s, fb=fb, qs=qs: e.copy(kz_b[0:64, 2 * fb, qs], fps[s][0:64, :]), reads=[B_fps[s]], writes=[B_qkTb[4 + fb][tc]])
                                P.op("act", lambda e, s=s, fb=fb, qs=qs: e.copy(kz_b[64:128, 2 * fb + 1, qs], fps[s][64:128, :]), reads=[B_fps[s]], writes=[B_qkTb[4 + fb][tc]])
                            else:
                                P.op("act", lambda e, s=s, fb=fb, qs=qs: e.activation(out=gT_b[:, fb, qs], in_=fps[s][:], func=ACTF.Silu),
                                     reads=[B_fps[s]], writes=[B_gTb[fb][tc]])
                    if grp == 1:
                        for t in range(NT):
                            s = t % 2
                            tok = slice(t * 128, (t + 1) * 128)
                            P.op("pe", [lambda e, s=s, kc=kc, tok=tok: e.matmul(vps[s][:], lhsT=xT[:, kc, tok], rhs=W_b[:, kc, 1024:1536],
                                                                                 start=(kc == 0), stop=(kc == 7)) for kc in range(8)],
                                 reads=[B_xT[t], B_Wb[2]], writes=[B_vps[s]])
                            P.op("dve", lambda e, s=s, t=t: e.tensor_copy(v_b[:, t, :, 0:64], vps[s][:].rearrange("p (h d) -> p h d", d=64)),
                                 reads=[B_vps[s]], writes=[B_vb[t]])
                P.flush()

            with ExitStack() as L34:
                w_o = W_b[:, 0:4, :].rearrange("p a (b c) -> p (a b) c", c=D)
                fg_bc = W_b[:, 4, :].bitcast(F32)
                B_wo, B_fg = Buf(), Buf()
                d_wo = dsem("d_wo")
                d_fg = dsem("d_fg")
                P.dma_group("pool", [lambda e, kc=kc: e.dma_start(out=w_o[:, kc, :], in_=wout_v[:, kc, :]) for kc in range(8)], B_wo, d_wo)
                P.op("sp", lambda e: e.dma_start(out=fg_bc, in_=fg_d.partition_broadcast(128)), writes=[B_fg], dma=d_fg)

                with ExitStack() as L3b:
                    accn = ps(L3b, "accn", [128, 4, 512], F32)
                    Sn = [ps(L3b, "Sn%d" % i, [128, 2, 512], F32) for i in range(2)]
                    NPB = 3
                    accs = xT[:, 0:2, :].bitcast(F32).rearrange("p a (b c) -> p (a b) c", c=512)
                    recn = xT[:, 2, :].bitcast(F32)
                    acc2 = xT[:, 3, :].bitcast(F32)
                    onn2 = [xT[:, 4, 0:1024].bitcast(F32), xT[:, 4, 1024:2048].bitcast(F32)]
                    B_onn2 = [Buf(), Buf()]
                    Pn = [xT[:, 5, 0:1024], xT[:, 5, 1024:2048], xT[:, 6, 0:1024]]
                    B_accn, B_accs, B_recn, B_onn, B_acc2 = Buf(), Buf(), Buf(), Buf(), Buf()
                    B_Sn = [Buf(), Buf()]
                    B_Pn = [Buf() for _ in range(NPB)]

                    def na_geom(j):
                        kr = (2 * j, 2 * j + 1)
                        rows = [r for r in range(ROWS) if any(_r0(r) <= k < _r0(r) + 8 for k in kr)]
                        ra, rb = rows[0], rows[-1] + 1
                        assert rows == list(range(ra, rb)) and rb - ra <= 16
                        n1 = (rb - ra + 1) // 2
                        chunks = [(ra, ra + n1), (ra + n1, rb)]
                        return kr, ra, rb, n1, chunks

                    def na_masks(j):
                        kr, ra, rb, n1, chunks = na_geom(j)
                        use_m, fix = [], []
                        for (ca, cb) in chunks:
                            true_v = {r: tuple(_r0(r) <= k < _r0(r) + 8 for k in kr) for r in range(ca, cb)}
                            rule_v = {}
                            for r in range(ca, cb):
                                b = r - 2 * j + 7
                                rule_v[r] = (True, False) if b == 4 else ((False, True) if b == 12 else (True, True))
                            if rule_v == true_v:
                                use_m.append(any(v != (True, True) for v in true_v.values()))
                            else:
                                use_m.append(False)
                                fix += [(r, v) for r, v in true_v.items() if v != (True, True)]
                        return use_m, fix

                    steps = [(h, j) for h in range(8) for j in range(NT)]
                    nst = len(steps)
                    zw = sb(L3b, "zw", [128, 128], BF16)
                    B_zw = Buf()
                    P.op("pool", lambda e: e.memset(zw[:], 0.0), writes=[B_zw])

                    def na_S(i):
                        h, j = steps[i]
                        p = h // 2
                        kr, ra, rb, n1, chunks = na_geom(j)
                        sl = i % 2
                        ts = h % 2
                        keys = slice(j * 128, (j + 1) * 128)
                        b0 = ra - 2 * j + 7
                        assert 0 <= b0 and b0 + 2 * n1 <= 16
                        use_m, _fix = na_masks(j)
                        if j == 0 and h + 1 < 8:
                            load_tab(h + 1)
                        mm = []
                        for ci, (ca, cb) in enumerate(chunks):
                            mm.append(lambda e, ci=ci, ca=ca, cb=cb: e.matmul(
                                Sn[sl][:, ci, 0:(cb - ca) * 64], lhsT=kz_b[:, h, keys], rhs=qT_b[:, p, ca * 64:cb * 64], start=True, stop=True))
                        for ci, (ca, cb) in enumerate(chunks):
                            bb = b0 + ci * n1
                            tb = tabm[ts] if use_m[ci] else tabb[ts]
                            mm.append(lambda e, ci=ci, bb=bb, tb=tb: e.matmul(
                                Sn[sl][:, ci, 0:n1 * 64], lhsT=ident[:], rhs=tb[:, bb:bb + n1, :].rearrange("p r c -> p (r c)"),
                                start=False, stop=True, skip_group_check=True))
                        P.op("pe", mm,
                             reads=[B_qkTb[4 + p][j // 4], B_kz0, B_ident, B_tabb[ts], B_tabm[ts]] + [B_qkTb[p][tc] for tc in range((ra * 64) // 512, ((rb * 64 - 1) // 512) + 1)],
                             writes=[B_Sn[sl]])

                    def na_fin_tail(h):
                        p, hh = h // 2, h % 2
                        ln = slice(hh * 64, hh * 64 + 64)
                        P.op("pool", lambda e: e.tensor_copy(recn[0:64, :], accs[64:128, 0:2, :].rearrange("p a b -> p (a b)")), reads=[B_accs], writes=[B_recn])
                        P.op("pool", lambda e: e.tensor_copy(recn[64:128, :], accs[64:128, 2:4, :].rearrange("p a b -> p (a b)")), reads=[B_accs], writes=[B_recn])
                        P.op("pool", lambda e: e.tensor_copy(acc2[64:128, :], accs[0:64, 2:4, :].rearrange("p a b -> p (a b)")), reads=[B_accs], writes=[B_acc2])
                        P.op("dve", lambda e: e.reciprocal(recn, recn), reads=[B_recn], writes=[B_recn])
                        for qb in range(4):
                            qs = slice(qb * 512, (qb + 1) * 512)
                            rc = slice((qb % 2) * 512, (qb % 2) * 512 + 512)
                            on_ = onn2[qb % 2]
                            Bon = B_onn2[qb % 2]
                            if qb < 2:
                                P.op("dve", lambda e, qb=qb, rc=rc, on_=on_: e.tensor_tensor(out=on_[ln, :], in0=accs[0:64, qb, :], in1=recn[0:64, rc], op=ALU.mult),
                                     reads=[B_accs, B_recn], writes=[Bon])
                            else:
                                P.op("dve", lambda e, qb=qb, rc=rc, on_=on_: e.tensor_tensor(out=on_[ln, :], in0=acc2[64:128, rc], in1=recn[64:128, rc], op=ALU.mult),
                                     reads=[B_acc2, B_recn], writes=[Bon])
                            P.op("pool", lambda e, qs=qs, on_=on_: e.tensor_tensor(out=gT_b[ln, p, qs], in0=on_[ln, :], in1=gT_b[ln, p, qs], op=ALU.mult),
                                 reads=[Bon, B_gTb[p][qb]], writes=[B_gTb[p][qb]])

                    def na_rest(i):
                        h, j = steps[i]
                        if j == 2 and h >= 1:
                            na_fin_tail(h - 1)
                        p, hh = h // 2, h % 2
                        ln = slice(hh * 64, hh * 64 + 64)
                        kr, ra, rb, n1, chunks = na_geom(j)
                        sl = i % 2
                        pl = i % NPB
                        w1 = n1 * 64
                        P.op("act", lambda e: e.activation(out=Pn[pl][:, 0:2 * w1].rearrange("p (a w) -> p a w", a=2), in_=Sn[sl][:, :, 0:w1], func=ACTF.Exp),
                             reads=[B_Sn[sl]], writes=[B_Pn[pl]])
                        for r, val in na_masks(j)[1]:
                            part = slice(64, 128) if val == (True, False) else slice(0, 64)
                            P.op("pool", lambda e, part=part, r=r: e.memset(Pn[pl][part, (r - ra) * 64:(r - ra + 1) * 64], 0.0),
                                 reads=[B_Pn[pl]], writes=[B_Pn[pl]])

                    def na_pv(i):
                        h, j = steps[i]
                        p, hh = h // 2, h % 2
                        ln = slice(hh * 64, hh * 64 + 64)
                        kr, ra, rb, n1, chunks = na_geom(j)
                        pl = i % NPB
                        mm = []
                        if j == 0:
                            for qb_ in range(4):
                                mm.append(lambda e, qb_=qb_: e.matmul(accn[:, qb_, :], lhsT=zw[:], rhs=qT_b[:, 0, 0:512], start=True, stop=True))
                        r = ra
                        while r < rb:
                            qb = r // 8
                            r_e = min(rb, (qb + 1) * 8)
                            mm.append(lambda e, qb=qb, r=r, r_e=r_e: e.matmul(
                                accn[:, qb, (r - 8 * qb) * 64:(r_e - 8 * qb) * 64], lhsT=v_b[:, j, h, :],
                                rhs=Pn[pl][:, (r - ra) * 64:(r_e - ra) * 64], start=False, stop=True, skip_group_check=True))
                            r = r_e
                        P.op("pe", mm, reads=[B_vb[j], B_vbones, B_Pn[pl]] + ([B_zw, B_qkTb[0][0]] if j == 0 else []), writes=[B_accn])
                        if j == NT - 1:
                            P.op("dve", lambda e: e.tensor_copy(accs, accn[:]), reads=[B_accn], writes=[B_accs])
                            if h == 7:
                                na_fin_tail(7)

                    na_S(0)
                    na_S(1)
                    for i in range(nst):
                        na_rest(i)
                        if i + 2 < nst:
                            na_S(i + 2)
                        na_pv(i)
                    P.flush()

                with ExitStack() as L4:
                    yps = [ps(L4, "yps%d" % i, [128, 2, 512], F32) for i in range(2)]
                    xr = [xT[:, k, :].bitcast(F32) for k in range(3)]
                    yt = [xT[:, 3 + k, :].bitcast(F32) for k in range(2)]
                    yo = [xT[:, 5 + k, :].bitcast(F32) for k in range(2)]
                    sq4 = [xT[:, 7, 0:1024], xT[:, 7, 1024:2048]]
                    st4 = sb(L4, "st4", [128, NT, 4], F32)
                    B_yps, B_xr, B_yt, B_yo, B_sq4 = ([Buf(), Buf(), Buf()] for _ in range(5))
                    B_st4 = [Buf() for _ in range(NT)]
                    B_st4all = Buf()
                    d_xr = [dsem("d_xr0"), dsem("d_xr1"), dsem("d_xr2")]
                    d_yo = [dsem("d_yo0"), dsem("d_yo1")]
                    P.op("dve", lambda e: e.memset(st4[:], 0.0), writes=[B_st4all])
                    def o_ld(t):
                        s3 = t % 3
                        P.op("sp", lambda e: e.dma_start(out=xr[s3][:], in_=x_d[t * 128:(t + 1) * 128, :]), writes=[B_xr[s3]], dma=d_xr[s3])

                    def o_mm(t):
                        s = t % 2
                        tok = slice(t * 128, (t + 1) * 128)
                        mm = []
                        for nh in range(2):
                            for fc in range(8):
                                src = gT_a if fc < 4 else gT_b
                                mm.append(lambda e, nh=nh, fc=fc, src=src: e.matmul(
                                    yps[s][:, nh, :], lhsT=src[:, fc % 4, tok], rhs=w_o[:, fc, nh * 512:(nh + 1) * 512], start=(fc == 0), stop=(fc == 7)))
                        P.op("pe", mm, reads=[B_wo] + [B_gTa[fb][t // 4] for fb in range(4)] + [B_gTb[fb][t // 4] for fb in range(4)], writes=[B_yps[s]])

                    def o_add(t):
                        s, s3 = t % 2, t % 3
                        P.op("dve", lambda e: e.tensor_tensor(out=yt[s][:], in0=yps[s][:].rearrange("p a b -> p (a b)"), in1=xr[s3][:], op=ALU.add),
                             reads=[B_yps[s], B_xr[s3]], writes=[B_yt[s]])

                    def o_sq(t):
                        s = t % 2
                        P.op("act", lambda e: e.activation(out=sq4[0][:], in_=yt[s][:], func=ACTF.Square, accum_out=st4[:, t, 0:1]),
                             reads=[B_yt[s], B_st4all], writes=[B_sq4[0], B_st4[t]])

                    def o_scale(t):
                        P.op("dve", lambda e: e.tensor_scalar(st4[:, t, 1:2], st4[:, t, 0:1], 1.0 / D, EPS, ALU.mult, ALU.add),
                             reads=[B_st4[t]], writes=[B_st4[t]])

                    def o_sqrt(t):
                        P.op("act", lambda e: e.activation(out=st4[:, t, 1:2], in_=st4[:, t, 1:2], func=ACTF.Sqrt), reads=[B_st4[t]], writes=[B_st4[t]])

                    def o_fin(t):
                        s = t % 2
                        tok = slice(t * 128, (t + 1) * 128)
                        P.op("dve", lambda e: e.reciprocal(st4[:, t, 2:3], st4[:, t, 1:2]), reads=[B_st4[t]], writes=[B_st4[t]])
                        P.op("dve", lambda e: e.scalar_tensor_tensor(out=yo[s][:], in0=yt[s][:], scalar=st4[:, t, 2:3], in1=fg_bc,
                                                                      op0=ALU.mult, op1=ALU.mult),
                             reads=[B_yt[s], B_st4[t], B_fg], writes=[B_yo[s]])
                        P.op("sp", lambda e: e.dma_start(out=y_d[tok, :], in_=yo[s][:]), reads=[B_yo[s]], dma=d_yo[s])

                    o_ld(0)
                    pipeline([(lambda t: o_ld(t + 1) if t + 1 < NT else None, 0), (o_mm, 0), (o_scale, 1), (o_sqrt, 1), (o_add, 0), (o_fin, 1), (o_sq, 0)], NT)
                    for d in d_yo:
                        P.wait("sp", (d.sem, d.count))
                    P.flush()
    return nc


def _rope_tables():
    t = np.arange(S_LEN)
    row = (t // GRID_W).astype(np.float32)
    col = (t % GRID_W).astype(np.float32)
    inv = (np.float32(10000.0) ** (-np.arange(16, dtype=np.float32) * np.float32(2.0 / 32))).astype(np.float32)
    ang_r = row[:, None] * inv[None, :]
    ang_c = col[:, None] * inv[None, :]
    ang = np.concatenate([ang_r, ang_r, ang_c, ang_c], axis=-1).astype(np.float32)
    cos = np.cos(ang).astype(np.float32)
    sin = np.sin(ang).astype(np.float32)
    sgn = np.ones(64, np.float32)
    sgn[0:16] = -1.0
    sgn[32:48] = -1.0
    ssin = sin * sgn[None, :]

    def lay(a):
        return np.ascontiguousarray(a.reshape(NT, 128, 64).transpose(1, 0, 2).reshape(128, NT * 64))
    return lay(cos), lay(ssin)


def _na_layout(rpb):
    krl = np.arange(128) // 64
    kc = np.arange(128) % 64
    b = np.arange(15)
    c = np.arange(64)
    a = np.clip(14 - b[None, :] + krl[:, None], 0, 14)
    dc = np.clip(kc[:, None] - c[None, :] + 15, 0, 30)
    g = rpb[:, a[:, :, None], dc[:, None, :]]
    g = np.ascontiguousarray(g.transpose(1, 0, 2, 3)).reshape(128, 8 * 15 * 64).astype(np.float32)
    cs = np.clip(c - 8, 0, GRID_W - 16)
    valid = (kc[:, None] >= cs[None, :]) & (kc[:, None] < cs[None, :] + 16)
    mask = np.where(valid, 0.0, NEG).astype(np.float32)
    return g, np.ascontiguousarray(mask)


_NC_CACHE = {}


def kernel(x, norm_gain, w_in, q_norm_a, k_norm_a, na_rpb, w_out, final_norm_gain):
    x = np.asarray(x, np.float32)
    n = x.shape[0]
    if "nc" not in _NC_CACHE:
        _NC_CACHE["nc"] = build_nc()
    nc = _NC_CACHE["nc"]
    cos_t, sin_t = _rope_tables()
    nab, nam = _na_layout(np.asarray(na_rpb, np.float32)[0])
    common = {
        "w_in": np.ascontiguousarray(np.asarray(w_in, np.float32)[0]),
        "w_out": np.ascontiguousarray(np.asarray(w_out, np.float32)[0]),
        "norm_gain": np.ascontiguousarray(np.asarray(norm_gain, np.float32)[0]),
        "final_norm_gain": np.ascontiguousarray(np.asarray(final_norm_gain, np.float32)),
        "q_norm_a": np.ascontiguousarray(np.asarray(q_norm_a, np.float32)[0]),
        "k_norm_a": np.ascontiguousarray(np.asarray(k_norm_a, np.float32)[0]),
        "cos_tab": cos_t, "sin_tab": sin_t, "na_bias": nab, "na_mask": nam,
    }
    in_maps = [dict(common, x=np.ascontiguousarray(x[b])) for b in range(n)]
    res = run_bass_kernel_spmd(nc, in_maps, core_ids=list(range(n)))
    return np.stack([np.asarray(r["y"], np.float32) for r in res.results], axis=0)
```

```python
import numpy as np
from contextlib import ExitStack
import concourse.bass as bass
import concourse.mybir as mybir
from concourse.bass_utils import run_bass_kernel_spmd

F32 = mybir.dt.float32
BF16 = mybir.dt.bfloat16
ACTF = mybir.ActivationFunctionType
ALU = mybir.AluOpType
AX = mybir.AxisListType

S_LEN = 2048
D = 1024
NT = 16
GRID_W = 64
ROWS = 32
EPS = 1e-6
NEG = -30000.0
D_IN = 3328


class Buf:
    __slots__ = ("name", "w", "r")

    def __init__(self, name=""):
        self.name = name
        self.w = None
        self.r = []


class DmaSem:
    def __init__(self, sem):
        self.sem = sem
        self.count = 0


class Prog:
    ENGS = ("pe", "act", "dve", "pool", "sp")

    def __init__(self, nc, sems):
        self.nc = nc
        self.sem = sems
        self.cnt = {e: 0 for e in self.ENGS}
        self.waited = {e: {} for e in self.ENGS}
        self.eng = {"pe": nc.tensor, "act": nc.scalar, "dve": nc.vector, "pool": nc.gpsimd, "sp": nc.sync}

    def _emit_wait(self, eng, ticket):
        if ticket is None:
            return
        sem, val = ticket
        key = id(sem)
        w = self.waited[eng]
        if w.get(key, 0) >= val:
            return
        w[key] = val
        self.eng[eng].wait_ge(sem, val)

    def op(self, eng, fns, reads=(), writes=(), extra_waits=(), dma=None):
        if callable(fns):
            fns = [fns]
        for b in reads:
            self._emit_wait(eng, b.w)
        for b in writes:
            self._emit_wait(eng, b.w)
            for t in b.r:
                self._emit_wait(eng, t)
        for t in extra_waits:
            self._emit_wait(eng, t)
        if dma is not None:
            dma.count += 16
            sem = dma.sem
            ticket = (sem, dma.count)
            inc = 16
        else:
            self.cnt[eng] += 1
            sem = self.sem[eng]
            ticket = (sem, self.cnt[eng])
            inc = 1
        e = self.eng[eng]
        n = len(fns)
        for i, fn in enumerate(fns):
            ins = fn(e)
            if i == n - 1:
                ins.then_inc(sem, inc)
        for b in reads:
            b.r.append(ticket)
        for b in writes:
            b.w = ticket
            b.r = []
        return ticket

    def wait(self, eng, ticket):
        self._emit_wait(eng, ticket)

    def dma_group(self, eng, fns, buf, dma):
        for t in ([buf.w] if buf.w else []) + buf.r:
            self._emit_wait(eng, t)
        e = self.eng[eng]
        for fn in fns:
            dma.count += 16
            fn(e).then_inc(dma.sem, 16)
        buf.w = (dma.sem, dma.count)
        buf.r = []
        return buf.w

    def fence(self, engines=None, on=("pe", "act", "dve", "pool")):
        for e in (engines or self.ENGS):
            for c in on:
                if c != e and self.cnt[c] > 0:
                    self._emit_wait(e, (self.sem[c], self.cnt[c]))

    def flush(self):
        self.fence()


def pipeline(stages, n):
    mx = max(sk for _, sk in stages)
    for step in range(n + mx):
        for f, sk in stages:
            i = step - sk
            if 0 <= i < n:
                f(i)


def _r0(r):
    return min(max(r - 4, 0), ROWS - 8)


def build_nc():
    nc = bass.Bass("TRN2", target_bir_lowering=False)
    x_d = nc.dram_tensor("x", [S_LEN, D], F32, kind="ExternalInput").ap()
    win_d = nc.dram_tensor("w_in", [D, D_IN], F32, kind="ExternalInput").ap()
    wout_d = nc.dram_tensor("w_out", [D, D], F32, kind="ExternalInput").ap()
    ng_d = nc.dram_tensor("norm_gain", [D], F32, kind="ExternalInput").ap()
    fg_d = nc.dram_tensor("final_norm_gain", [D], F32, kind="ExternalInput").ap()
    qg_d = nc.dram_tensor("q_norm_a", [64], F32, kind="ExternalInput").ap()
    kg_d = nc.dram_tensor("k_norm_a", [64], F32, kind="ExternalInput").ap()
    cos_d = nc.dram_tensor("cos_tab", [128, NT * 64], F32, kind="ExternalInput").ap()
    sin_d = nc.dram_tensor("sin_tab", [128, NT * 64], F32, kind="ExternalInput").ap()
    nab_d = nc.dram_tensor("na_bias", [128, 8 * 15 * 64], F32, kind="ExternalInput").ap()
    nam_d = nc.dram_tensor("na_mask", [128, 64], F32, kind="ExternalInput").ap()
    y_d = nc.dram_tensor("y", [S_LEN, D], F32, kind="ExternalOutput").ap()

    win_v = win_d.rearrange("(kc p) n -> p kc n", p=128)
    wout_v = wout_d.rearrange("(kc p) n -> p kc n", p=128)

    with ExitStack() as L0:
        def sb(es, name, shape, dt):
            return es.enter_context(nc.sbuf_tensor(name, shape, dt))

        def ps(es, name, shape, dt):
            return es.enter_context(nc.psum_tensor(name, shape, dt))

        sems = {e: L0.enter_context(nc.semaphore("s_" + e)) for e in Prog.ENGS}
        P = Prog(nc, sems)

        def dsem(name):
            return DmaSem(L0.enter_context(nc.semaphore(name)))

        gT_a = sb(L0, "gT_a", [128, 4, S_LEN], BF16)
        gT_b = sb(L0, "gT_b", [128, 4, S_LEN], BF16)
        xT = sb(L0, "xT", [128, 8, S_LEN], BF16)
        W_b = sb(L0, "W_b", [128, 8, 2048], BF16)
        B_Wb = [Buf() for _ in range(4)]
        ident = sb(L0, "ident", [128, 128], BF16)
        idf = sb(L0, "idf", [128, 128], F32)
        B_gTa = [[Buf() for _ in range(4)] for _ in range(4)]
        B_gTb = [[Buf() for _ in range(4)] for _ in range(4)]
        B_xT = [Buf() for _ in range(NT)]
        B_ident = Buf()
        B_idf = Buf()

        P.op("pool", lambda e: e.memset(idf[:], 0.0), writes=[B_idf])
        P.op("pool", lambda e: e.affine_select(out=idf[:], in_=idf[:], compare_op=ALU.not_equal, fill=1.0,
                                               base=0, pattern=[[-1, 128]], channel_multiplier=1),
             reads=[B_idf], writes=[B_idf])
        P.op("dve", lambda e: e.tensor_copy(ident[:], idf[:]), reads=[B_idf], writes=[B_ident])

        with ExitStack() as LA:
            qkT_a = sb(LA, "qkT_a", [128, 8, S_LEN], BF16)
            v_a = sb(LA, "v_a", [128, NT, 2, 128], BF16)
            v_a2 = sb(LA, "v_a2", [128, NT, 2, 128], BF16)
            B_qkTa = [Buf() for _ in range(NT)]
            B_va = [Buf() for _ in range(NT)]
            B_vaones = Buf()
            P.op("pool", lambda e: e.memset(v_a[:, :, :, 64:128], 1.0), writes=[B_vaones])
            P.op("pool", lambda e: e.memset(v_a2[:, :, :, 0:64], 1.0), writes=[B_vaones])

            with ExitStack() as L2:
                W_a = sb(L2, "W_a", [128, 8, 1280], BF16)
                B_Wa = [Buf(), Buf(), Buf()]
                d_wa = [dsem("d_wa%d" % i) for i in range(3)]
                col_rng = [(0, 512), (512, 768), (768, 1280)]
                def load_wa(pi):
                    c0, c1 = col_rng[pi]
                    P.dma_group("pool", [lambda e, kc=kc: e.dma_start(out=W_a[:, kc, c0:c1], in_=win_v[:, kc, c0:c1])
                                         for kc in range(8)], B_Wa[pi], d_wa[pi])

                load_wa(0)
                load_wa(1)

                gain_bc = sb(L2, "gain_bc", [128, D], F32)
                qg_bc = sb(L2, "qg_bc", [128, 64], F32)
                kg_bc = sb(L2, "kg_bc", [128, 64], F32)
                cgq = sb(L2, "cgq", [128, NT, 64], F32)
                sgq = sb(L2, "sgq", [128, NT, 64], F32)
                cgk = sb(L2, "cgk", [128, NT, 64], F32)
                sgk = sb(L2, "sgk", [128, NT, 64], F32)
                B_gain, B_qg, B_kg = Buf(), Buf(), Buf()
                B_cgq, B_sgq, B_cgk, B_sgk = Buf(), Buf(), Buf(), Buf()
                d_c = [dsem("d_c%d" % i) for i in range(3)]
                P.op("sp", lambda e: e.dma_start(out=gain_bc[:], in_=ng_d.partition_broadcast(128)), writes=[B_gain], dma=d_c[0])
                P.op("sp", lambda e: e.dma_start(out=qg_bc[:], in_=qg_d.partition_broadcast(128)), writes=[B_qg], dma=d_c[1])
                P.op("sp", lambda e: e.dma_start(out=kg_bc[:], in_=kg_d.partition_broadcast(128)), writes=[B_kg], dma=d_c[2])

                with ExitStack() as L1:
                    xs = [sb(L1, "xs%d" % i, [128, D], F32) for i in range(4)]
                    xn = [sb(L1, "xn%d" % i, [128, D], BF16) for i in range(2)]
                    sqj = [sb(L1, "sqj%d" % i, [128, D], BF16) for i in range(1)]
                    stat = sb(L1, "stat", [128, NT, 4], F32)
                    tp = [ps(L1, "tp%d" % i, [128, 8, 128], BF16) for i in range(2)]
                    B_xs, B_xn, B_sqj, B_tp = [Buf(), Buf(), Buf(), Buf()], [Buf(), Buf()], [Buf(), Buf()], [Buf(), Buf()]
                    B_stat = [Buf() for _ in range(NT)]
                    B_statall = Buf()
                    d_x = [dsem("d_x%d" % i) for i in range(4)]
                    d_t = [dsem("d_t%d" % i) for i in range(4)]

                    def load_tabs():
                        for ti, (tl, src, Bt) in enumerate(((cgq, cos_d, B_cgq), (sgq, sin_d, B_sgq), (cgk, cos_d, B_cgk), (sgk, sin_d, B_sgk))):
                            P.op("act", lambda e, tl=tl, src=src: e.dma_start(out=tl[:].rearrange("p t d -> p (t d)"), in_=src[:, :]), writes=[Bt], dma=d_t[ti])
                    P.op("dve", lambda e: e.memset(stat[:], 0.0), writes=[B_statall])

                    def fold_tables(cg, sg, g_bc, B_cg, B_sg, B_g, scale):
                        g4 = g_bc[:].rearrange("p (c h e) -> p c h e", c=2, h=2)
                        P.op("dve", lambda e: e.tensor_tensor(out=cg[:], in0=cg[:], in1=g_bc[:, None, :].broadcast_to([128, NT, 64]), op=ALU.mult),
                             reads=[B_cg, B_g], writes=[B_cg])
                        sg5 = sg[:].rearrange("p t (c h e) -> p t c h e", c=2, h=2)
                        for hf in range(2):
                            P.op("dve", lambda e, hf=hf: e.tensor_tensor(
                                out=sg5[:, :, :, hf, :], in0=sg5[:, :, :, hf, :],
                                in1=g4[:, None, :, 1 - hf, :].broadcast_to([128, NT, 2, 16]), op=ALU.mult),
                                reads=[B_sg, B_g], writes=[B_sg])
                        if scale != 1.0:
                            P.op("dve", lambda e: e.tensor_scalar(cg[:], cg[:], scale, None, ALU.mult), reads=[B_cg], writes=[B_cg])
                            P.op("dve", lambda e: e.tensor_scalar(sg[:], sg[:], scale, None, ALU.mult), reads=[B_sg], writes=[B_sg])


                    def p1_load(t):
                        s3 = t % 4
                        tk = P.op("sp", lambda e: e.dma_start(out=xs[s3][:], in_=x_d[t * 128:(t + 1) * 128, :]), writes=[B_xs[s3]], dma=d_x[s3])
                        if t == 8:
                            P.wait("pool", tk)
                            load_wa(2)
                            load_tabs()

                    def p1_sq(t):
                        s3 = t % 4
                        P.op("act", lambda e: e.activation(out=sqj[0][:], in_=xs[s3][:], func=ACTF.Square, accum_out=stat[:, t, 0:1]),
                             reads=[B_xs[s3], B_statall], writes=[B_sqj[0], B_stat[t]])

                    def p1_scale(t):
                        P.op("dve", lambda e: e.tensor_scalar(stat[:, t, 1:2], stat[:, t, 0:1], 1.0 / D, EPS, ALU.mult, ALU.add),
                             reads=[B_stat[t]], writes=[B_stat[t]])

                    def p1_sqrt(t):
                        P.op("act", lambda e: e.activation(out=stat[:, t, 1:2], in_=stat[:, t, 1:2], func=ACTF.Sqrt), reads=[B_stat[t]], writes=[B_stat[t]])

                    def p1_norm(t):
                        s3, s = t % 4, t % 2
                        P.op("dve", lambda e: e.reciprocal(stat[:, t, 2:3], stat[:, t, 1:2]), reads=[B_stat[t]], writes=[B_stat[t]])
                        P.op("dve", lambda e: e.scalar_tensor_tensor(out=xn[s][:], in0=xs[s3][:], scalar=stat[:, t, 2:3], in1=gain_bc[:],
                                                                      op0=ALU.mult, op1=ALU.mult),
                             reads=[B_xs[s3], B_stat[t], B_gain], writes=[B_xn[s]])

                    def p1_tr(t):
                        s = t % 2
                        P.op("pe", [lambda e, kc=kc: e.transpose(tp[s][:, kc, :], xn[s][:, kc * 128:(kc + 1) * 128], ident[:]) for kc in range(8)],
                             reads=[B_xn[s], B_ident], writes=[B_tp[s]])

                    def p1_evac(t):
                        s = t % 2
                        if t % 2 == 0:
                            P.op("act", lambda e: e.copy(xT[:, :, t * 128:(t + 1) * 128], tp[s][:]), reads=[B_tp[s]], writes=[B_xT[t]])
                        else:
                            P.op("dve", lambda e: e.tensor_copy(xT[:, :, t * 128:(t + 1) * 128], tp[s][:]), reads=[B_tp[s]], writes=[B_xT[t]])

                    pipeline([(p1_load, 0), (p1_sq, 0), (p1_sqrt, 1), (p1_norm, 1), (p1_tr, 1), (p1_evac, 2), (p1_scale, 0)], NT)
                    P.flush()
                    fold_tables(cgq, sgq, qg_bc, B_cgq, B_sgq, B_qg, 0.125)
                    fold_tables(cgk, sgk, kg_bc, B_cgk, B_sgk, B_kg, 1.0)

                with ExitStack() as L2a:
                    qps = [ps(L2a, "qps%d" % i, [128, 512], F32) for i in range(2)]
                    kvps = [ps(L2a, "kvps%d" % i, [128, 512], F32) for i in range(2)]
                    tpq = [ps(L2a, "tpq%d" % i, [128, 8, 128], BF16) for i in range(2)]
                    gps = [ps(L2a, "gps%d" % i, [128, 512], F32) for i in range(2)]
                    sqf = [sb(L2a, "sqf%d" % i, [128, 640], F32) for i in range(2)]
                    rs = [sb(L2a, "rs%d" % i, [128, 10], F32) for i in range(2)]
                    t1 = [sb(L2a, "t1_%d" % i, [128, 640], F32) for i in range(2)]
                    t2 = [sb(L2a, "t2_%d" % i, [128, 640], F32) for i in range(2)]
                    qkr = [sb(L2a, "qkr%d" % i, [128, 1024], BF16) for i in range(2)]
                    B_qkz = [Buf(), Buf()]
                    for i_ in range(2):
                        P.op("pool", lambda e, i_=i_: e.memset(qkr[i_][:, 512:1024], 0.0), writes=[B_qkz[i_]])
                    B_qps, B_kvps, B_tpq, B_gps = [Buf(), Buf()], [Buf(), Buf()], [Buf(), Buf()], [Buf(), Buf()]
                    B_sqf, B_rs, B_qhat, B_t1, B_t2, B_qkr = ([Buf(), Buf()] for _ in range(6))

                    def kz_dst(tile_):
                        base = tile_[:, 512:1024]
                        return bass.AP(base.tensor, base.offset, [list(base.ap[0]), [256, 2], [192, 2], [1, 64]])

                    def a_mm(t):
                        s = t % 2
                        tok = slice(t * 128, (t + 1) * 128)
                        P.op("pe", [lambda e, kc=kc: e.matmul(qps[s][:], lhsT=xT[:, kc, tok], rhs=W_a[:, kc, 0:512],
                                                               start=(kc == 0), stop=(kc == 7)) for kc in range(8)],
                             reads=[B_xT[t], B_Wa[0]], writes=[B_qps[s]])
                        P.op("pe", [lambda e, kc=kc: e.matmul(kvps[s][:, 0:256], lhsT=xT[:, kc, tok], rhs=W_a[:, kc, 512:768],
                                                               start=(kc == 0), stop=(kc == 7)) for kc in range(8)],
                             reads=[B_xT[t], B_Wa[1]], writes=[B_kvps[s]])

                    sq_tk = {}

                    def a_sq(t):
                        s = t % 2
                        P.op("act", lambda e: e.activation(out=sqf[s][:, 0:512], in_=qps[s][:], func=ACTF.Square), reads=[B_qps[s]], writes=[B_sqf[s]])
                        sq_tk[t] = P.op("act", lambda e: e.activation(out=sqf[s][:, 512:640], in_=kvps[s][:, 0:128], func=ACTF.Square),
                                        reads=[B_kvps[s]], writes=[B_sqf[s]])

                    def a_rope(t):
                        s = t % 2
                        P.wait("dve", sq_tk[t])
                        for (lo, hi, nh, cg, sg, Bc, Bs, src, Bsrc) in ((0, 512, 8, cgq, sgq, B_cgq, B_sgq, qps[s][:, 0:512], B_qps[s]),
                                                                          (512, 640, 2, cgk, sgk, B_cgk, B_sgk, kvps[s][:, 0:128], B_kvps[s])):
                            P.op("dve", lambda e, lo=lo, hi=hi, nh=nh, cg=cg, src=src: e.tensor_tensor(
                                out=t1[s][:, lo:hi].rearrange("p (h d) -> p h d", d=64),
                                in0=src.rearrange("p (h d) -> p h d", d=64),
                                in1=cg[:, t, None, :].broadcast_to([128, nh, 64]), op=ALU.mult),
                                reads=[Bsrc, Bc], writes=[B_t1[s]])
                            for hf in range(2):
                                P.op("dve", lambda e, lo=lo, hi=hi, nh=nh, sg=sg, hf=hf, src=src: e.tensor_tensor(
                                    out=t2[s][:, lo:hi].rearrange("p (h c f e) -> p h c f e", c=2, f=2, e=16)[:, :, :, hf, :],
                                    in0=src.rearrange("p (h c f e) -> p h c f e", c=2, f=2, e=16)[:, :, :, 1 - hf, :],
                                    in1=sg[:, t, :].rearrange("p (c f e) -> p c f e", c=2, f=2)[:, None, :, hf, :].broadcast_to([128, nh, 2, 16]),
                                    op=ALU.mult),
                                    reads=[Bsrc, Bs], writes=[B_t2[s]])
                        P.op("dve", lambda e: e.tensor_copy(v_a[:, t, :, 0:64], kvps[s][:, 128:256].rearrange("p (h d) -> p h d", d=64)),
                             reads=[B_kvps[s]], writes=[B_va[t]])
                        P.op("dve", lambda e: e.tensor_copy(v_a2[:, t, :, 64:128], kvps[s][:, 128:256].rearrange("p (h d) -> p h d", d=64)),
                             reads=[B_kvps[s]], writes=[B_va[t]])

                    def a_red(t):
                        s = t % 2
                        P.op("dve", lambda e: e.tensor_reduce(out=rs[s][:], in_=sqf[s][:].rearrange("p (h d) -> p h d", d=64), axis=AX.X, op=ALU.add),
                             reads=[B_sqf[s]], writes=[B_rs[s]])
                        P.op("dve", lambda e: e.tensor_scalar(rs[s][:], rs[s][:], 1.0 / 64, EPS, ALU.mult, ALU.add), reads=[B_rs[s]], writes=[B_rs[s]])

                    def a_sqrt(t):
                        s = t % 2
                        P.op("act", lambda e: e.activation(out=rs[s][:], in_=rs[s][:], func=ACTF.Sqrt), reads=[B_rs[s]], writes=[B_rs[s]])

                    def a_rec(t):
                        s = t % 2
                        P.op("dve", lambda e: e.reciprocal(rs[s][:], rs[s][:]), reads=[B_rs[s]], writes=[B_rs[s]])

                    def a_comb(t):
                        s = t % 2
                        P.op("pool", lambda e: e.tensor_tensor(out=t1[s][:], in0=t1[s][:], in1=t2[s][:], op=ALU.add),
                             reads=[B_t1[s], B_t2[s]], writes=[B_t1[s]])
                        P.op("pool", lambda e: e.tensor_tensor(out=qkr[s][:, 0:512].rearrange("p (h d) -> p h d", d=64),
                                                               in0=t1[s][:, 0:512].rearrange("p (h d) -> p h d", d=64),
                                                               in1=rs[s][:, 0:8, None].broadcast_to([128, 8, 64]), op=ALU.mult),
                             reads=[B_t1[s], B_rs[s]], writes=[B_qkr[s]])
                        P.op("pool", lambda e: e.tensor_tensor(
                            out=kz_dst(qkr[s]),
                            in0=t1[s][:, 512:640].rearrange("p (k d) -> p k d", k=2)[:, :, None, :].broadcast_to([128, 2, 2, 64]),
                            in1=rs[s][:, 8:10, None, None].broadcast_to([128, 2, 2, 64]), op=ALU.mult),
                             reads=[B_t1[s], B_rs[s], B_qkz[s]], writes=[B_qkr[s]])

                    def a_tr(t):
                        s = t % 2
                        P.op("pe", [lambda e, i=i: e.transpose(tpq[s][:, i, :], qkr[s][:, i * 128:(i + 1) * 128], ident[:]) for i in range(8)],
                             reads=[B_qkr[s], B_ident], writes=[B_tpq[s]])

                    def a_ev(t):
                        s = t % 2
                        P.op("act", lambda e: e.copy(qkT_a[:, :, t * 128:(t + 1) * 128], tpq[s][:]), reads=[B_tpq[s]], writes=[B_qkTa[t]])

                    def gate_grp(g):
                        return divmod(g, 4)

                    def a_gate_mm(t):
                        if t % 2 == 0:
                            return
                        for g in (t - 1, t):
                            fb, tc = gate_grp(g)
                            s = g % 2
                            P.op("pe", [lambda e, s=s, kc=kc, fb=fb, tc=tc: e.matmul(gps[s][:], lhsT=W_a[:, kc, 768 + fb * 128:768 + (fb + 1) * 128],
                                                                                        rhs=xT[:, kc, tc * 512:(tc + 1) * 512], start=(kc == 0), stop=(kc == 7))
                                        for kc in range(8)],
                                 reads=[B_Wa[2]] + B_xT[tc * 4:(tc + 1) * 4], writes=[B_gps[s]])

                    def a_gate_act(t):
                        if t % 2 == 0:
                            return
                        for g in (t - 1, t):
                            fb, tc = gate_grp(g)
                            s = g % 2
                            P.op("act", lambda e, s=s, fb=fb, tc=tc: e.activation(out=gT_a[:, fb, tc * 512:(tc + 1) * 512], in_=gps[s][:], func=ACTF.Silu),
                                 reads=[B_gps[s]], writes=[B_gTa[fb][tc]])

                    pipeline([(a_mm, 0), (a_sqrt, 2), (a_sq, 0), (a_rope, 1), (a_red, 1), (a_rec, 2), (a_comb, 2), (a_tr, 3), (a_ev, 3),
                              (a_gate_act, 1), (a_gate_mm, 0)], NT)
                    P.flush()

            with ExitStack() as L3a:
                NS = 3
                Sps = [ps(L3a, "Sps%d" % i, [128, 2, 512], F32) for i in range(NS)]
                acc = ps(L3a, "acc", [128, 2, 512], F32)
                NPT = 3
                Pt = [sb(L3a, "Pt%d" % i, [128, 2, 512], BF16) for i in range(NPT)]
                accs = [sb(L3a, "accs%d" % i, [128, 2, 512], F32) for i in range(2)]
                rec = [sb(L3a, "rec%d" % i, [128, 512], F32) for i in range(2)]
                onm = [sb(L3a, "onm%d" % i, [128, 512], F32) for i in range(2)]
                B_S = [Buf() for _ in range(NS)]
                B_acc = Buf()
                B_accs, B_rec, B_onm = [Buf(), Buf()], [Buf(), Buf()], [Buf(), Buf()]
                B_Pt = [Buf() for _ in range(NPT)]

                d_wb = [dsem("d_wb%d" % i) for i in range(4)]
                for pi in range(4):
                    P.dma_group("pool", [lambda e, kc=kc, pi=pi: e.dma_start(out=W_b[:, kc, pi * 512:(pi + 1) * 512],
                                                                              in_=win_v[:, kc, 1280 + pi * 512:1280 + (pi + 1) * 512])
                                         for kc in range(8)], B_Wb[pi], d_wb[pi])

                steps = [(p, c, kt) for p in range(4) for c in range(4) for kt in range(NT)]

                def emit_S(i):
                    p, c, kt = steps[i]
                    kv = p // 2
                    sl = i % NS
                    keys = slice(kt * 128, (kt + 1) * 128)
                    qs = slice(c * 512, (c + 1) * 512)
                    P.op("pe", [lambda e: e.matmul(Sps[sl][:, 0, :], lhsT=qkT_a[:, 4 + 2 * kv, keys], rhs=qkT_a[:, p, qs], start=True, stop=True),
                                lambda e: e.matmul(Sps[sl][:, 1, :], lhsT=qkT_a[:, 5 + 2 * kv, keys], rhs=qkT_a[:, p, qs], start=True, stop=True)],
                         reads=[B_qkTa[kt]] + B_qkTa[c * 4:(c + 1) * 4], writes=[B_S[sl]])

                def emit_exp(i):
                    sl = i % NS
                    pl = i % NPT
                    P.op("act", lambda e: e.activation(out=Pt[pl][:], in_=Sps[sl][:], func=ACTF.Exp), reads=[B_S[sl]], writes=[B_Pt[pl]])

                def emit_PV(i):
                    p, c, kt = steps[i]
                    kv = p // 2
                    pl = i % NPT
                    al = (i // NT) % 2
                    P.op("pe", [lambda e: e.matmul(acc[:, 0, :], lhsT=v_a[:, kt, kv, :], rhs=Pt[pl][:, 0, :], start=(kt == 0), stop=(kt == NT - 1)),
                                lambda e: e.matmul(acc[:, 1, :], lhsT=v_a2[:, kt, kv, :], rhs=Pt[pl][:, 1, :], start=(kt == 0), stop=(kt == NT - 1))],
                         reads=[B_va[kt], B_vaones, B_Pt[pl]], writes=[B_acc])
                    if kt == NT - 1:
                        qs = slice(c * 512, (c + 1) * 512)
                        P.op("dve", lambda e: e.tensor_copy(accs[al][:], acc[:]), reads=[B_acc], writes=[B_accs[al]])
                        P.op("dve", lambda e: e.reciprocal(rec[al][0:64, :], accs[al][64:128, 0, :]), reads=[B_accs[al]], writes=[B_rec[al]])
                        P.op("dve", lambda e: e.reciprocal(rec[al][64:128, :], accs[al][0:64, 1, :]), reads=[B_accs[al]], writes=[B_rec[al]])
                        P.op("dve", lambda e: e.tensor_tensor(out=onm[al][0:64, :], in0=accs[al][0:64, 0, :], in1=rec[al][0:64, :], op=ALU.mult),
                             reads=[B_accs[al], B_rec[al]], writes=[B_onm[al]])
                        P.op("dve", lambda e: e.tensor_tensor(out=onm[al][64:128, :], in0=accs[al][64:128, 1, :], in1=rec[al][64:128, :], op=ALU.mult),
                             reads=[B_accs[al], B_rec[al]], writes=[B_onm[al]])
                        P.op("pool", lambda e: e.tensor_tensor(out=gT_a[:, p, qs], in0=onm[al][:], in1=gT_a[:, p, qs], op=ALU.mult),
                             reads=[B_onm[al], B_gTa[p][c]], writes=[B_gTa[p][c]])

                n = len(steps)
                emit_S(0)
                emit_S(1)
                for i in range(n):
                    if i + 2 < n:
                        emit_S(i + 2)
                    emit_exp(i)
                    emit_PV(i)
                P.fence(engines=("act", "dve", "pool", "sp"))

        with ExitStack() as LB:
            qT_b = sb(LB, "qT_b", [128, 4, S_LEN], BF16)
            kz_b = sb(LB, "kz_b", [128, 8, S_LEN], BF16)
            B_kz0 = Buf()
            kz4 = kz_b[:].rearrange("p (a b) s -> p a b s", b=2)
            P.op("pool", lambda e: e.memset(kz4[64:128, :, 0, :], 0.0), writes=[B_kz0])
            P.op("pool", lambda e: e.memset(kz4[0:64, :, 1, :], 0.0), writes=[B_kz0])
            v_b = sb(LB, "v_b", [128, NT, 8, 128], BF16)
            nam = sb(LB, "na_msk", [128, 64], F32)
            B_qkTb = [[Buf() for _ in range(4)] for _ in range(8)]
            B_vb = [Buf() for _ in range(NT)]
            B_vbones, B_nam = Buf(), Buf()
            d_nam = dsem("d_nam")
            P.op("pool", lambda e: e.memset(v_b[:, :, :, 64:128], 1.0), writes=[B_vbones])
            P.op("sp", lambda e: e.dma_start(out=nam[:], in_=nam_d[:, :]), writes=[B_nam], dma=d_nam)
            tabh = [sb(LB, "tabh%d" % i, [128, 15, 64], F32) for i in range(2)]
            tabb = [sb(LB, "tabb%d" % i, [128, 16, 64], BF16) for i in range(2)]
            tabm = [sb(LB, "tabm%d" % i, [128, 16, 64], BF16) for i in range(2)]
            B_tabh, B_tabb, B_tabm = [Buf(), Buf()], [Buf(), Buf()], [Buf(), Buf()]
            d_tab = [dsem("d_tab0"), dsem("d_tab1")]

            def load_tab(h):
                ts = h % 2
                P.op("sp", lambda e: e.dma_start(out=tabh[ts][:].rearrange("p b c -> p (b c)"), in_=nab_d[:, h * 960:(h + 1) * 960]),
                     writes=[B_tabh[ts]], dma=d_tab[ts])
                P.op("pool", lambda e: e.tensor_tensor(out=tabb[ts][:, 0:15, :], in0=tabh[ts][:], in1=nam[:, None, :].broadcast_to([128, 15, 64]), op=ALU.add),
                     reads=[B_tabh[ts], B_nam], writes=[B_tabb[ts]])
                P.op("pool", lambda e: e.tensor_copy(tabm[ts][:, 0:15, :], tabb[ts][:, 0:15, :]), reads=[B_tabb[ts]], writes=[B_tabm[ts]])
                P.op("pool", lambda e: e.memset(tabm[ts][64:128, 4, :], NEG), writes=[B_tabm[ts]])
                P.op("pool", lambda e: e.memset(tabm[ts][0:64, 12, :], NEG), writes=[B_tabm[ts]])

            for ts_ in range(2):
                P.op("pool", lambda e, ts_=ts_: e.memset(tabb[ts_][:, 15:16, :], 0.0), writes=[B_tabb[ts_]])
                P.op("pool", lambda e, ts_=ts_: e.memset(tabm[ts_][:, 15:16, :], 0.0), writes=[B_tabm[ts_]])
            load_tab(0)

            with ExitStack() as L2b:
                NFP = 6
                fps = [ps(L2b, "fps%d" % i, [128, 512], F32) for i in range(NFP)]
                vps = [ps(L2b, "vps%d" % i, [128, 512], F32) for i in range(2)]
                B_fps = [Buf() for _ in range(NFP)]
                B_vps = [Buf(), Buf()]
                fi = 0
                for grp in (0, 1, 3):
                    for fb in range(4):
                        for tc in range(4):
                            s = fi % NFP
                            fi += 1
                            col0 = grp * 512 + fb * 128
                            P.op("pe", [lambda e, s=s, kc=kc, col0=col0, tc=tc: e.matmul(fps[s][:], lhsT=W_b[:, kc, col0:col0 + 128],
                                                                                          rhs=xT[:, kc, tc * 512:(tc + 1) * 512], start=(kc == 0), stop=(kc == 7))
                                        for kc in range(8)],
                                 reads=[B_Wb[grp]] + B_xT[tc * 4:(tc + 1) * 4], writes=[B_fps[s]])
                            qs = slice(tc * 512, (tc + 1) * 512)
                            if grp == 0:
                                P.op("dve", lambda e, s=s, fb=fb, qs=qs: e.tensor_scalar(qT_b[:, fb, qs], fps[s][:], 0.125, None, ALU.mult),
                                     reads=[B_fps[s]], writes=[B_qkTb[fb][tc]])
                            elif grp == 1:
                                P.op("act", lambda e, s=s, fb=fb, qs=qs: e.copy(kz_b[0:64, 2 * fb, qs], fps[s][0:64, :]), reads=[B_fps[s]], writes=[B_qkTb[4 + fb][tc]])
                                P.op("act", lambda e, s=s, fb=fb, qs=qs: e.copy(kz_b[64:128, 2 * fb + 1, qs], fps[s][64:128, :]), reads=[B_fps[s]], writes=[B_qkTb[4 + fb][tc]])
                            else:
                                P.op("act", lambda e, s=s, fb=fb, qs=qs: e.activation(out=gT_b[:, fb, qs], in_=fps[s][:], func=ACTF.Silu),
                                     reads=[B_fps[s]], writes=[B_gTb[fb][tc]])
                    if grp == 1:
                        for t in range(NT):
                            s = t % 2
                            tok = slice(t * 128, (t + 1) * 128)
                            P.op("pe", [lambda e, s=s, kc=kc, tok=tok: e.matmul(vps[s][:], lhsT=xT[:, kc, tok], rhs=W_b[:, kc, 1024:1536],
                                                                                 start=(kc == 0), stop=(kc == 7)) for kc in range(8)],
                                 reads=[B_xT[t], B_Wb[2]], writes=[B_vps[s]])
                            P.op("dve", lambda e, s=s, t=t: e.tensor_copy(v_b[:, t, :, 0:64], vps[s][:].rearrange("p (h d) -> p h d", d=64)),
                                 reads=[B_vps[s]], writes=[B_vb[t]])
                P.flush()

            with ExitStack() as L34:
                w_o = W_b[:, 0:4, :].rearrange("p a (b c) -> p (a b) c", c=D)
                fg_bc = W_b[:, 4, :].bitcast(F32)
                B_wo, B_fg = Buf(), Buf()
                d_wo = dsem("d_wo")
                d_fg = dsem("d_fg")
                P.dma_group("pool", [lambda e, kc=kc: e.dma_start(out=w_o[:, kc, :], in_=wout_v[:, kc, :]) for kc in range(8)], B_wo, d_wo)
                P.op("sp", lambda e: e.dma_start(out=fg_bc, in_=fg_d.partition_broadcast(128)), writes=[B_fg], dma=d_fg)

                with ExitStack() as L3b:
                    accn = ps(L3b, "accn", [128, 4, 512], F32)
                    Sn = [ps(L3b, "Sn%d" % i, [128, 2, 512], F32) for i in range(2)]
                    NPB = 3
                    accs = xT[:, 0:2, :].bitcast(F32).rearrange("p a (b c) -> p (a b) c", c=512)
                    recn = xT[:, 2, :].bitcast(F32)
                    acc2 = xT[:, 3, :].bitcast(F32)
                    onn2 = [xT[:, 4, 0:1024].bitcast(F32), xT[:, 4, 1024:2048].bitcast(F32)]
                    B_onn2 = [Buf(), Buf()]
                    Pn = [xT[:, 5, 0:1024], xT[:, 5, 1024:2048], xT[:, 6, 0:1024]]
                    B_accn, B_accs, B_recn, B_onn, B_acc2 = Buf(), Buf(), Buf(), Buf(), Buf()
                    B_Sn = [Buf(), Buf()]
                    B_Pn = [Buf() for _ in range(NPB)]

                    def na_geom(j):
                        kr = (2 * j, 2 * j + 1)
                        rows = [r for r in range(ROWS) if any(_r0(r) <= k < _r0(r) + 8 for k in kr)]
                        ra, rb = rows[0], rows[-1] + 1
                        assert rows == list(range(ra, rb)) and rb - ra <= 16
                        n1 = (rb - ra + 1) // 2
                        chunks = [(ra, ra + n1), (ra + n1, rb)]
                        return kr, ra, rb, n1, chunks

                    def na_masks(j):
                        kr, ra, rb, n1, chunks = na_geom(j)
                        use_m, fix = [], []
                        for (ca, cb) in chunks:
                            true_v = {r: tuple(_r0(r) <= k < _r0(r) + 8 for k in kr) for r in range(ca, cb)}
                            rule_v = {}
                            for r in range(ca, cb):
                                b = r - 2 * j + 7
                                rule_v[r] = (True, False) if b == 4 else ((False, True) if b == 12 else (True, True))
                            if rule_v == true_v:
                                use_m.append(any(v != (True, True) for v in true_v.values()))
                            else:
                                use_m.append(False)
                                fix += [(r, v) for r, v in true_v.items() if v != (True, True)]
                        return use_m, fix

                    steps = [(h, j) for h in range(8) for j in range(NT)]
                    nst = len(steps)

                    def na_S(i):
                        h, j = steps[i]
                        p = h // 2
                        kr, ra, rb, n1, chunks = na_geom(j)
                        sl = i % 2
                        ts = h % 2
                        keys = slice(j * 128, (j + 1) * 128)
                        b0 = ra - 2 * j + 7
                        assert 0 <= b0 and b0 + 2 * n1 <= 16
                        use_m, _fix = na_masks(j)
                        if j == 0 and h + 1 < 8:
                            load_tab(h + 1)
                        mm = []
                        for ci, (ca, cb) in enumerate(chunks):
                            mm.append(lambda e, ci=ci, ca=ca, cb=cb: e.matmul(
                                Sn[sl][:, ci, 0:(cb - ca) * 64], lhsT=kz_b[:, h, keys], rhs=qT_b[:, p, ca * 64:cb * 64], start=True, stop=True))
                        for ci, (ca, cb) in enumerate(chunks):
                            bb = b0 + ci * n1
                            tb = tabm[ts] if use_m[ci] else tabb[ts]
                            mm.append(lambda e, ci=ci, bb=bb, tb=tb: e.matmul(
                                Sn[sl][:, ci, 0:n1 * 64], lhsT=ident[:], rhs=tb[:, bb:bb + n1, :].rearrange("p r c -> p (r c)"),
                                start=False, stop=True, skip_group_check=True))
                        P.op("pe", mm,
                             reads=[B_qkTb[4 + p][j // 4], B_kz0, B_ident, B_tabb[ts], B_tabm[ts]] + [B_qkTb[p][tc] for tc in range((ra * 64) // 512, ((rb * 64 - 1) // 512) + 1)],
                             writes=[B_Sn[sl]])

                    def na_rest(i):
                        h, j = steps[i]
                        p, hh = h // 2, h % 2
                        ln = slice(hh * 64, hh * 64 + 64)
                        kr, ra, rb, n1, chunks = na_geom(j)
                        sl = i % 2
                        pl = i % NPB
                        w1 = n1 * 64
                        if i == 0:
                            P.op("dve", lambda e: e.memset(accn[:], 0.0), writes=[B_accn])
                        P.op("act", lambda e: e.activation(out=Pn[pl][:, 0:2 * w1].rearrange("p (a w) -> p a w", a=2), in_=Sn[sl][:, :, 0:w1], func=ACTF.Exp),
                             reads=[B_Sn[sl]], writes=[B_Pn[pl]])
                        for r, val in na_masks(j)[1]:
                            part = slice(64, 128) if val == (True, False) else slice(0, 64)
                            P.op("pool", lambda e, part=part, r=r: e.memset(Pn[pl][part, (r - ra) * 64:(r - ra + 1) * 64], 0.0),
                                 reads=[B_Pn[pl]], writes=[B_Pn[pl]])

                    def na_pv(i):
                        h, j = steps[i]
                        p, hh = h // 2, h % 2
                        ln = slice(hh * 64, hh * 64 + 64)
                        kr, ra, rb, n1, chunks = na_geom(j)
                        pl = i % NPB
                        mm = []
                        r = ra
                        while r < rb:
                            qb = r // 8
                            r_e = min(rb, (qb + 1) * 8)
                            mm.append(lambda e, qb=qb, r=r, r_e=r_e: e.matmul(
                                accn[:, qb, (r - 8 * qb) * 64:(r_e - 8 * qb) * 64], lhsT=v_b[:, j, h, :],
                                rhs=Pn[pl][:, (r - ra) * 64:(r_e - ra) * 64], start=False, stop=True, skip_group_check=True))
                            r = r_e
                        P.op("pe", mm, reads=[B_vb[j], B_vbones, B_Pn[pl]], writes=[B_accn])
                        if j == NT - 1:
                            P.op("dve", lambda e: e.tensor_copy(accs, accn[:]), reads=[B_accn], writes=[B_accs])
                            if h + 1 < 8:
                                P.op("dve", lambda e: e.memset(accn[:], 0.0), writes=[B_accn])
                            P.op("dve", lambda e: e.tensor_copy(recn[0:64, :], accs[64:128, 0:2, :].rearrange("p a b -> p (a b)")), reads=[B_accs], writes=[B_recn])
                            P.op("dve", lambda e: e.tensor_copy(recn[64:128, :], accs[64:128, 2:4, :].rearrange("p a b -> p (a b)")), reads=[B_accs], writes=[B_recn])
                            P.op("dve", lambda e: e.tensor_copy(acc2[64:128, :], accs[0:64, 2:4, :].rearrange("p a b -> p (a b)")), reads=[B_accs], writes=[B_acc2])
                            P.op("dve", lambda e: e.reciprocal(recn, recn), reads=[B_recn], writes=[B_recn])
                            for qb in range(4):
                                qs = slice(qb * 512, (qb + 1) * 512)
                                rc = slice((qb % 2) * 512, (qb % 2) * 512 + 512)
                                on_ = onn2[qb % 2]
                                Bon = B_onn2[qb % 2]
                                if qb < 2:
                                    P.op("dve", lambda e, qb=qb, rc=rc, on_=on_: e.tensor_tensor(out=on_[ln, :], in0=accs[0:64, qb, :], in1=recn[0:64, rc], op=ALU.mult),
                                         reads=[B_accs, B_recn], writes=[Bon])
                                else:
                                    P.op("dve", lambda e, qb=qb, rc=rc, on_=on_: e.tensor_tensor(out=on_[ln, :], in0=acc2[64:128, rc], in1=recn[64:128, rc], op=ALU.mult),
                                         reads=[B_acc2, B_recn], writes=[Bon])
                                P.op("pool", lambda e, qs=qs, on_=on_: e.tensor_tensor(out=gT_b[ln, p, qs], in0=on_[ln, :], in1=gT_b[ln, p, qs], op=ALU.mult),
                                     reads=[Bon, B_gTb[p][qb]], writes=[B_gTb[p][qb]])

                    na_S(0)
                    na_S(1)
                    for i in range(nst):
                        na_rest(i)
                        if i + 2 < nst:
                            na_S(i + 2)
                        na_pv(i)
                    P.flush()

                with ExitStack() as L4:
                    yps = [ps(L4, "yps%d" % i, [128, 2, 512], F32) for i in range(2)]
                    xr = [xT[:, k, :].bitcast(F32) for k in range(3)]
                    yt = [xT[:, 3 + k, :].bitcast(F32) for k in range(2)]
                    yo = [xT[:, 5 + k, :].bitcast(F32) for k in range(2)]
                    sq4 = [xT[:, 7, 0:1024], xT[:, 7, 1024:2048]]
                    st4 = sb(L4, "st4", [128, NT, 4], F32)
                    B_yps, B_xr, B_yt, B_yo, B_sq4 = ([Buf(), Buf(), Buf()] for _ in range(5))
                    B_st4 = [Buf() for _ in range(NT)]
                    B_st4all = Buf()
                    d_xr = [dsem("d_xr0"), dsem("d_xr1"), dsem("d_xr2")]
                    d_yo = [dsem("d_yo0"), dsem("d_yo1")]
                    P.op("dve", lambda e: e.memset(st4[:], 0.0), writes=[B_st4all])
                    def o_ld(t):
                        s3 = t % 3
                        P.op("sp", lambda e: e.dma_start(out=xr[s3][:], in_=x_d[t * 128:(t + 1) * 128, :]), writes=[B_xr[s3]], dma=d_xr[s3])

                    def o_mm(t):
                        s = t % 2
                        tok = slice(t * 128, (t + 1) * 128)
                        mm = []
                        for nh in range(2):
                            for fc in range(8):
                                src = gT_a if fc < 4 else gT_b
                                mm.append(lambda e, nh=nh, fc=fc, src=src: e.matmul(
                                    yps[s][:, nh, :], lhsT=src[:, fc % 4, tok], rhs=w_o[:, fc, nh * 512:(nh + 1) * 512], start=(fc == 0), stop=(fc == 7)))
                        P.op("pe", mm, reads=[B_wo] + [B_gTa[fb][t // 4] for fb in range(4)] + [B_gTb[fb][t // 4] for fb in range(4)], writes=[B_yps[s]])

                    def o_add(t):
                        s, s3 = t % 2, t % 3
                        P.op("dve", lambda e: e.tensor_tensor(out=yt[s][:], in0=yps[s][:].rearrange("p a b -> p (a b)"), in1=xr[s3][:], op=ALU.add),
                             reads=[B_yps[s], B_xr[s3]], writes=[B_yt[s]])

                    def o_sq(t):
                        s = t % 2
                        P.op("act", lambda e: e.activation(out=sq4[0][:], in_=yt[s][:], func=ACTF.Square, accum_out=st4[:, t, 0:1]),
                             reads=[B_yt[s], B_st4all], writes=[B_sq4[0], B_st4[t]])

                    def o_scale(t):
                        P.op("dve", lambda e: e.tensor_scalar(st4[:, t, 1:2], st4[:, t, 0:1], 1.0 / D, EPS, ALU.mult, ALU.add),
                             reads=[B_st4[t]], writes=[B_st4[t]])

                    def o_sqrt(t):
                        P.op("act", lambda e: e.activation(out=st4[:, t, 1:2], in_=st4[:, t, 1:2], func=ACTF.Sqrt), reads=[B_st4[t]], writes=[B_st4[t]])

                    def o_fin(t):
                        s = t % 2
                        tok = slice(t * 128, (t + 1) * 128)
                        P.op("dve", lambda e: e.reciprocal(st4[:, t, 2:3], st4[:, t, 1:2]), reads=[B_st4[t]], writes=[B_st4[t]])
                        P.op("dve", lambda e: e.scalar_tensor_tensor(out=yo[s][:], in0=yt[s][:], scalar=st4[:, t, 2:3], in1=fg_bc,
                                                                      op0=ALU.mult, op1=ALU.mult),
                             reads=[B_yt[s], B_st4[t], B_fg], writes=[B_yo[s]])
                        P.op("sp", lambda e: e.dma_start(out=y_d[tok, :], in_=yo[s][:]), reads=[B_yo[s]], dma=d_yo[s])

                    o_ld(0)
                    pipeline([(lambda t: o_ld(t + 1) if t + 1 < NT else None, 0), (o_mm, 0), (o_scale, 1), (o_sqrt, 1), (o_add, 0), (o_fin, 1), (o_sq, 0)], NT)
                    for d in d_yo:
                        P.wait("sp", (d.sem, d.count))
                    P.flush()
    return nc


def _rope_tables():
    t = np.arange(S_LEN)
    row = (t // GRID_W).astype(np.float32)
    col = (t % GRID_W).astype(np.float32)
    inv = (np.float32(10000.0) ** (-np.arange(16, dtype=np.float32) * np.float32(2.0 / 32))).astype(np.float32)
    ang_r = row[:, None] * inv[None, :]
    ang_c = col[:, None] * inv[None, :]
    ang = np.concatenate([ang_r, ang_r, ang_c, ang_c], axis=-1).astype(np.float32)
    cos = np.cos(ang).astype(np.float32)
    sin = np.sin(ang).astype(np.float32)
    sgn = np.ones(64, np.float32)
    sgn[0:16] = -1.0
    sgn[32:48] = -1.0
    ssin = sin * sgn[None, :]

    def lay(a):
        return np.ascontiguousarray(a.reshape(NT, 128, 64).transpose(1, 0, 2).reshape(128, NT * 64))
    return lay(cos), lay(ssin)


def _na_layout(rpb):
    krl = np.arange(128) // 64
    kc = np.arange(128) % 64
    b = np.arange(15)
    c = np.arange(64)
    a = np.clip(14 - b[None, :] + krl[:, None], 0, 14)
    dc = np.clip(kc[:, None] - c[None, :] + 15, 0, 30)
    g = rpb[:, a[:, :, None], dc[:, None, :]]
    g = np.ascontiguousarray(g.transpose(1, 0, 2, 3)).reshape(128, 8 * 15 * 64).astype(np.float32)
    cs = np.clip(c - 8, 0, GRID_W - 16)
    valid = (kc[:, None] >= cs[None, :]) & (kc[:, None] < cs[None, :] + 16)
    mask = np.where(valid, 0.0, NEG).astype(np.float32)
    return g, np.ascontiguousarray(mask)


_NC_CACHE = {}


def kernel(x, norm_gain, w_in, q_norm_a, k_norm_a, na_rpb, w_out, final_norm_gain):
    x = np.asarray(x, np.float32)
    n = x.shape[0]
    if "nc" not in _NC_CACHE:
        _NC_CACHE["nc"] = build_nc()
    nc = _NC_CACHE["nc"]
    cos_t, sin_t = _rope_tables()
    nab, nam = _na_layout(np.asarray(na_rpb, np.float32)[0])
    common = {
        "w_in": np.ascontiguousarray(np.asarray(w_in, np.float32)[0]),
        "w_out": np.ascontiguousarray(np.asarray(w_out, np.float32)[0]),
        "norm_gain": np.ascontiguousarray(np.asarray(norm_gain, np.float32)[0]),
        "final_norm_gain": np.ascontiguousarray(np.asarray(final_norm_gain, np.float32)),
        "q_norm_a": np.ascontiguousarray(np.asarray(q_norm_a, np.float32)[0]),
        "k_norm_a": np.ascontiguousarray(np.asarray(k_norm_a, np.float32)[0]),
        "cos_tab": cos_t, "sin_tab": sin_t, "na_bias": nab, "na_mask": nam,
    }
    in_maps = [dict(common, x=np.ascontiguousarray(x[b])) for b in range(n)]
    res = run_bass_kernel_spmd(nc, in_maps, core_ids=list(range(n)))
    return np.stack([np.asarray(r["y"], np.float32) for r in res.results], axis=0)
```

```python
import numpy as np
from contextlib import ExitStack
import concourse.bass as bass
import concourse.mybir as mybir
from concourse.bass_utils import run_bass_kernel_spmd

F32 = mybir.dt.float32
BF16 = mybir.dt.bfloat16
ACTF = mybir.ActivationFunctionType
ALU = mybir.AluOpType
AX = mybir.AxisListType

S_LEN = 2048
D = 1024
NT = 16
GRID_W = 64
ROWS = 32
EPS = 1e-6
NEG = -30000.0
D_IN = 3328


class Buf:
    __slots__ = ("name", "w", "r")

    def __init__(self, name=""):
        self.name = name
        self.w = None
        self.r = []


class DmaSem:
    def __init__(self, sem):
        self.sem = sem
        self.count = 0


class Prog:
    ENGS = ("pe", "act", "dve", "pool", "sp")

    def __init__(self, nc, sems):
        self.nc = nc
        self.sem = sems
        self.cnt = {e: 0 for e in self.ENGS}
        self.waited = {e: {} for e in self.ENGS}
        self.eng = {"pe": nc.tensor, "act": nc.scalar, "dve": nc.vector, "pool": nc.gpsimd, "sp": nc.sync}

    def _emit_wait(self, eng, ticket):
        if ticket is None:
            return
        sem, val = ticket
        key = id(sem)
        w = self.waited[eng]
        if w.get(key, 0) >= val:
            return
        w[key] = val
        self.eng[eng].wait_ge(sem, val)

    def op(self, eng, fns, reads=(), writes=(), extra_waits=(), dma=None):
        if callable(fns):
            fns = [fns]
        for b in reads:
            self._emit_wait(eng, b.w)
        for b in writes:
            self._emit_wait(eng, b.w)
            for t in b.r:
                self._emit_wait(eng, t)
        for t in extra_waits:
            self._emit_wait(eng, t)
        if dma is not None:
            dma.count += 16
            sem = dma.sem
            ticket = (sem, dma.count)
            inc = 16
        else:
            self.cnt[eng] += 1
            sem = self.sem[eng]
            ticket = (sem, self.cnt[eng])
            inc = 1
        e = self.eng[eng]
        n = len(fns)
        for i, fn in enumerate(fns):
            ins = fn(e)
            if i == n - 1:
                ins.then_inc(sem, inc)
        for b in reads:
            b.r.append(ticket)
        for b in writes:
            b.w = ticket
            b.r = []
        return ticket

    def wait(self, eng, ticket):
        self._emit_wait(eng, ticket)

    def dma_group(self, eng, fns, buf, dma):
        for t in ([buf.w] if buf.w else []) + buf.r:
            self._emit_wait(eng, t)
        e = self.eng[eng]
        for fn in fns:
            dma.count += 16
            fn(e).then_inc(dma.sem, 16)
        buf.w = (dma.sem, dma.count)
        buf.r = []
        return buf.w

    def fence(self, engines=None, on=("pe", "act", "dve", "pool")):
        for e in (engines or self.ENGS):
            for c in on:
                if c != e and self.cnt[c] > 0:
                    self._emit_wait(e, (self.sem[c], self.cnt[c]))

    def flush(self):
        self.fence()


def pipeline(stages, n):
    mx = max(sk for _, sk in stages)
    for step in range(n + mx):
        for f, sk in stages:
            i = step - sk
            if 0 <= i < n:
                f(i)


def _r0(r):
    return min(max(r - 4, 0), ROWS - 8)


def build_nc():
    nc = bass.Bass("TRN2", target_bir_lowering=False)
    x_d = nc.dram_tensor("x", [S_LEN, D], F32, kind="ExternalInput").ap()
    win_d = nc.dram_tensor("w_in", [D, D_IN], F32, kind="ExternalInput").ap()
    wout_d = nc.dram_tensor("w_out", [D, D], F32, kind="ExternalInput").ap()
    ng_d = nc.dram_tensor("norm_gain", [D], F32, kind="ExternalInput").ap()
    fg_d = nc.dram_tensor("final_norm_gain", [D], F32, kind="ExternalInput").ap()
    qg_d = nc.dram_tensor("q_norm_a", [64], F32, kind="ExternalInput").ap()
    kg_d = nc.dram_tensor("k_norm_a", [64], F32, kind="ExternalInput").ap()
    cos_d = nc.dram_tensor("cos_tab", [128, NT * 64], F32, kind="ExternalInput").ap()
    sin_d = nc.dram_tensor("sin_tab", [128, NT * 64], F32, kind="ExternalInput").ap()
    nab_d = nc.dram_tensor("na_bias", [128, 8 * 15 * 64], F32, kind="ExternalInput").ap()
    nam_d = nc.dram_tensor("na_mask", [128, 64], F32, kind="ExternalInput").ap()
    y_d = nc.dram_tensor("y", [S_LEN, D], F32, kind="ExternalOutput").ap()

    win_v = win_d.rearrange("(kc p) n -> p kc n", p=128)
    wout_v = wout_d.rearrange("(kc p) n -> p kc n", p=128)

    with ExitStack() as L0:
        def sb(es, name, shape, dt):
            return es.enter_context(nc.sbuf_tensor(name, shape, dt))

        def ps(es, name, shape, dt):
            return es.enter_context(nc.psum_tensor(name, shape, dt))

        sems = {e: L0.enter_context(nc.semaphore("s_" + e)) for e in Prog.ENGS}
        P = Prog(nc, sems)

        def dsem(name):
            return DmaSem(L0.enter_context(nc.semaphore(name)))

        gT_a = sb(L0, "gT_a", [128, 4, S_LEN], BF16)
        gT_b = sb(L0, "gT_b", [128, 4, S_LEN], BF16)
        xT = sb(L0, "xT", [128, 8, S_LEN], BF16)
        W_b = sb(L0, "W_b", [128, 8, 2048], BF16)
        B_Wb = [Buf() for _ in range(4)]
        ident = sb(L0, "ident", [128, 128], BF16)
        idf = sb(L0, "idf", [128, 128], F32)
        B_gTa = [[Buf() for _ in range(4)] for _ in range(4)]
        B_gTb = [[Buf() for _ in range(4)] for _ in range(4)]
        B_xT = [Buf() for _ in range(NT)]
        B_ident = Buf()
        B_idf = Buf()

        P.op("pool", lambda e: e.memset(idf[:], 0.0), writes=[B_idf])
        P.op("pool", lambda e: e.affine_select(out=idf[:], in_=idf[:], compare_op=ALU.not_equal, fill=1.0,
                                               base=0, pattern=[[-1, 128]], channel_multiplier=1),
             reads=[B_idf], writes=[B_idf])
        P.op("dve", lambda e: e.tensor_copy(ident[:], idf[:]), reads=[B_idf], writes=[B_ident])

        with ExitStack() as LA:
            qkT_a = sb(LA, "qkT_a", [128, 8, S_LEN], BF16)
            v_a = sb(LA, "v_a", [128, NT, 2, 128], BF16)
            v_a2 = sb(LA, "v_a2", [128, NT, 2, 128], BF16)
            B_qkTa = [Buf() for _ in range(NT)]
            B_va = [Buf() for _ in range(NT)]
            B_vaones = Buf()
            P.op("pool", lambda e: e.memset(v_a[:, :, :, 64:128], 1.0), writes=[B_vaones])
            P.op("pool", lambda e: e.memset(v_a2[:, :, :, 0:64], 1.0), writes=[B_vaones])

            with ExitStack() as L2:
                W_a = sb(L2, "W_a", [128, 8, 1280], BF16)
                B_Wa = [Buf(), Buf(), Buf()]
                d_wa = [dsem("d_wa%d" % i) for i in range(3)]
                col_rng = [(0, 512), (512, 768), (768, 1280)]
                def load_wa(pi):
                    c0, c1 = col_rng[pi]
                    P.dma_group("pool", [lambda e, kc=kc: e.dma_start(out=W_a[:, kc, c0:c1], in_=win_v[:, kc, c0:c1])
                                         for kc in range(8)], B_Wa[pi], d_wa[pi])

                load_wa(0)
                load_wa(1)

                gain_bc = sb(L2, "gain_bc", [128, D], F32)
                qg_bc = sb(L2, "qg_bc", [128, 64], F32)
                kg_bc = sb(L2, "kg_bc", [128, 64], F32)
                cgq = sb(L2, "cgq", [128, NT, 64], F32)
                sgq = sb(L2, "sgq", [128, NT, 64], F32)
                cgk = sb(L2, "cgk", [128, NT, 64], F32)
                sgk = sb(L2, "sgk", [128, NT, 64], F32)
                B_gain, B_qg, B_kg = Buf(), Buf(), Buf()
                B_cgq, B_sgq, B_cgk, B_sgk = Buf(), Buf(), Buf(), Buf()
                d_c = [dsem("d_c%d" % i) for i in range(3)]
                P.op("sp", lambda e: e.dma_start(out=gain_bc[:], in_=ng_d.partition_broadcast(128)), writes=[B_gain], dma=d_c[0])
                P.op("sp", lambda e: e.dma_start(out=qg_bc[:], in_=qg_d.partition_broadcast(128)), writes=[B_qg], dma=d_c[1])
                P.op("sp", lambda e: e.dma_start(out=kg_bc[:], in_=kg_d.partition_broadcast(128)), writes=[B_kg], dma=d_c[2])

                with ExitStack() as L1:
                    xs = [sb(L1, "xs%d" % i, [128, D], F32) for i in range(4)]
                    xn = [sb(L1, "xn%d" % i, [128, D], BF16) for i in range(2)]
                    sqj = [sb(L1, "sqj%d" % i, [128, D], BF16) for i in range(1)]
                    stat = sb(L1, "stat", [128, NT, 4], F32)
                    tp = [ps(L1, "tp%d" % i, [128, 8, 128], BF16) for i in range(2)]
                    B_xs, B_xn, B_sqj, B_tp = [Buf(), Buf(), Buf(), Buf()], [Buf(), Buf()], [Buf(), Buf()], [Buf(), Buf()]
                    B_stat = [Buf() for _ in range(NT)]
                    B_statall = Buf()
                    d_x = [dsem("d_x%d" % i) for i in range(4)]
                    d_t = [dsem("d_t%d" % i) for i in range(4)]

                    def load_tabs():
                        for ti, (tl, src, Bt) in enumerate(((cgq, cos_d, B_cgq), (sgq, sin_d, B_sgq), (cgk, cos_d, B_cgk), (sgk, sin_d, B_sgk))):
                            P.op("act", lambda e, tl=tl, src=src: e.dma_start(out=tl[:].rearrange("p t d -> p (t d)"), in_=src[:, :]), writes=[Bt], dma=d_t[ti])
                    P.op("dve", lambda e: e.memset(stat[:], 0.0), writes=[B_statall])

                    def fold_tables(cg, sg, g_bc, B_cg, B_sg, B_g, scale):
                        g4 = g_bc[:].rearrange("p (c h e) -> p c h e", c=2, h=2)
                        P.op("dve", lambda e: e.tensor_tensor(out=cg[:], in0=cg[:], in1=g_bc[:, None, :].broadcast_to([128, NT, 64]), op=ALU.mult),
                             reads=[B_cg, B_g], writes=[B_cg])
                        sg5 = sg[:].rearrange("p t (c h e) -> p t c h e", c=2, h=2)
                        for hf in range(2):
                            P.op("dve", lambda e, hf=hf: e.tensor_tensor(
                                out=sg5[:, :, :, hf, :], in0=sg5[:, :, :, hf, :],
                                in1=g4[:, None, :, 1 - hf, :].broadcast_to([128, NT, 2, 16]), op=ALU.mult),
                                reads=[B_sg, B_g], writes=[B_sg])
                        if scale != 1.0:
                            P.op("dve", lambda e: e.tensor_scalar(cg[:], cg[:], scale, None, ALU.mult), reads=[B_cg], writes=[B_cg])
                            P.op("dve", lambda e: e.tensor_scalar(sg[:], sg[:], scale, None, ALU.mult), reads=[B_sg], writes=[B_sg])


                    def p1_load(t):
                        s3 = t % 4
                        tk = P.op("sp", lambda e: e.dma_start(out=xs[s3][:], in_=x_d[t * 128:(t + 1) * 128, :]), writes=[B_xs[s3]], dma=d_x[s3])
                        if t == 8:
                            P.wait("pool", tk)
                            load_wa(2)
                            load_tabs()

                    def p1_sq(t):
                        s3 = t % 4
                        P.op("act", lambda e: e.activation(out=sqj[0][:], in_=xs[s3][:], func=ACTF.Square, accum_out=stat[:, t, 0:1]),
                             reads=[B_xs[s3], B_statall], writes=[B_sqj[0], B_stat[t]])

                    def p1_scale(t):
                        P.op("dve", lambda e: e.tensor_scalar(stat[:, t, 1:2], stat[:, t, 0:1], 1.0 / D, EPS, ALU.mult, ALU.add),
                             reads=[B_stat[t]], writes=[B_stat[t]])

                    def p1_sqrt(t):
                        P.op("act", lambda e: e.activation(out=stat[:, t, 1:2], in_=stat[:, t, 1:2], func=ACTF.Sqrt), reads=[B_stat[t]], writes=[B_stat[t]])

                    def p1_norm(t):
                        s3, s = t % 4, t % 2
                        P.op("dve", lambda e: e.reciprocal(stat[:, t, 2:3], stat[:, t, 1:2]), reads=[B_stat[t]], writes=[B_stat[t]])
                        P.op("dve", lambda e: e.scalar_tensor_tensor(out=xn[s][:], in0=xs[s3][:], scalar=stat[:, t, 2:3], in1=gain_bc[:],
                                                                      op0=ALU.mult, op1=ALU.mult),
                             reads=[B_xs[s3], B_stat[t], B_gain], writes=[B_xn[s]])

                    def p1_tr(t):
                        s = t % 2
                        P.op("pe", [lambda e, kc=kc: e.transpose(tp[s][:, kc, :], xn[s][:, kc * 128:(kc + 1) * 128], ident[:]) for kc in range(8)],
                             reads=[B_xn[s], B_ident], writes=[B_tp[s]])

                    def p1_evac(t):
                        s = t % 2
                        if t % 2 == 0:
                            P.op("act", lambda e: e.copy(xT[:, :, t * 128:(t + 1) * 128], tp[s][:]), reads=[B_tp[s]], writes=[B_xT[t]])
                        else:
                            P.op("dve", lambda e: e.tensor_copy(xT[:, :, t * 128:(t + 1) * 128], tp[s][:]), reads=[B_tp[s]], writes=[B_xT[t]])

                    pipeline([(p1_load, 0), (p1_sq, 0), (p1_sqrt, 1), (p1_norm, 1), (p1_tr, 1), (p1_evac, 2), (p1_scale, 0)], NT)
                    P.flush()
                    fold_tables(cgq, sgq, qg_bc, B_cgq, B_sgq, B_qg, 0.125)
                    fold_tables(cgk, sgk, kg_bc, B_cgk, B_sgk, B_kg, 1.0)

                with ExitStack() as L2a:
                    qps = [ps(L2a, "qps%d" % i, [128, 512], F32) for i in range(2)]
                    kvps = [ps(L2a, "kvps%d" % i, [128, 512], F32) for i in range(2)]
                    tpq = [ps(L2a, "tpq%d" % i, [128, 8, 128], BF16) for i in range(2)]
                    gps = [ps(L2a, "gps%d" % i, [128, 512], F32) for i in range(2)]
                    sqf = [sb(L2a, "sqf%d" % i, [128, 640], F32) for i in range(2)]
                    rs = [sb(L2a, "rs%d" % i, [128, 10], F32) for i in range(2)]
                    t1 = [sb(L2a, "t1_%d" % i, [128, 640], F32) for i in range(2)]
                    t2 = [sb(L2a, "t2_%d" % i, [128, 640], F32) for i in range(2)]
                    qkr = [sb(L2a, "qkr%d" % i, [128, 1024], BF16) for i in range(2)]
                    B_qkz = [Buf(), Buf()]
                    for i_ in range(2):
                        P.op("pool", lambda e, i_=i_: e.memset(qkr[i_][:, 512:1024], 0.0), writes=[B_qkz[i_]])
                    B_qps, B_kvps, B_tpq, B_gps = [Buf(), Buf()], [Buf(), Buf()], [Buf(), Buf()], [Buf(), Buf()]
                    B_sqf, B_rs, B_qhat, B_t1, B_t2, B_qkr = ([Buf(), Buf()] for _ in range(6))

                    def kz_dst(tile_):
                        base = tile_[:, 512:1024]
                        return bass.AP(base.tensor, base.offset, [list(base.ap[0]), [256, 2], [192, 2], [1, 64]])

                    def a_mm(t):
                        s = t % 2
                        tok = slice(t * 128, (t + 1) * 128)
                        P.op("pe", [lambda e, kc=kc: e.matmul(qps[s][:], lhsT=xT[:, kc, tok], rhs=W_a[:, kc, 0:512],
                                                               start=(kc == 0), stop=(kc == 7)) for kc in range(8)],
                             reads=[B_xT[t], B_Wa[0]], writes=[B_qps[s]])
                        P.op("pe", [lambda e, kc=kc: e.matmul(kvps[s][:, 0:256], lhsT=xT[:, kc, tok], rhs=W_a[:, kc, 512:768],
                                                               start=(kc == 0), stop=(kc == 7)) for kc in range(8)],
                             reads=[B_xT[t], B_Wa[1]], writes=[B_kvps[s]])

                    sq_tk = {}

                    def a_sq(t):
                        s = t % 2
                        P.op("act", lambda e: e.activation(out=sqf[s][:, 0:512], in_=qps[s][:], func=ACTF.Square), reads=[B_qps[s]], writes=[B_sqf[s]])
                        sq_tk[t] = P.op("act", lambda e: e.activation(out=sqf[s][:, 512:640], in_=kvps[s][:, 0:128], func=ACTF.Square),
                                        reads=[B_kvps[s]], writes=[B_sqf[s]])

                    def a_rope(t):
                        s = t % 2
                        P.wait("dve", sq_tk[t])
                        for (lo, hi, nh, cg, sg, Bc, Bs, src, Bsrc) in ((0, 512, 8, cgq, sgq, B_cgq, B_sgq, qps[s][:, 0:512], B_qps[s]),
                                                                          (512, 640, 2, cgk, sgk, B_cgk, B_sgk, kvps[s][:, 0:128], B_kvps[s])):
                            P.op("dve", lambda e, lo=lo, hi=hi, nh=nh, cg=cg, src=src: e.tensor_tensor(
                                out=t1[s][:, lo:hi].rearrange("p (h d) -> p h d", d=64),
                                in0=src.rearrange("p (h d) -> p h d", d=64),
                                in1=cg[:, t, None, :].broadcast_to([128, nh, 64]), op=ALU.mult),
                                reads=[Bsrc, Bc], writes=[B_t1[s]])
                            for hf in range(2):
                                P.op("dve", lambda e, lo=lo, hi=hi, nh=nh, sg=sg, hf=hf, src=src: e.tensor_tensor(
                                    out=t2[s][:, lo:hi].rearrange("p (h c f e) -> p h c f e", c=2, f=2, e=16)[:, :, :, hf, :],
                                    in0=src.rearrange("p (h c f e) -> p h c f e", c=2, f=2, e=16)[:, :, :, 1 - hf, :],
                                    in1=sg[:, t, :].rearrange("p (c f e) -> p c f e", c=2, f=2)[:, None, :, hf, :].broadcast_to([128, nh, 2, 16]),
                                    op=ALU.mult),
                                    reads=[Bsrc, Bs], writes=[B_t2[s]])
                        P.op("dve", lambda e: e.tensor_copy(v_a[:, t, :, 0:64], kvps[s][:, 128:256].rearrange("p (h d) -> p h d", d=64)),
                             reads=[B_kvps[s]], writes=[B_va[t]])
                        P.op("dve", lambda e: e.tensor_copy(v_a2[:, t, :, 64:128], kvps[s][:, 128:256].rearrange("p (h d) -> p h d", d=64)),
                             reads=[B_kvps[s]], writes=[B_va[t]])

                    def a_red(t):
                        s = t % 2
                        P.op("dve", lambda e: e.tensor_reduce(out=rs[s][:], in_=sqf[s][:].rearrange("p (h d) -> p h d", d=64), axis=AX.X, op=ALU.add),
                             reads=[B_sqf[s]], writes=[B_rs[s]])
                        P.op("dve", lambda e: e.tensor_scalar(rs[s][:], rs[s][:], 1.0 / 64, EPS, ALU.mult, ALU.add), reads=[B_rs[s]], writes=[B_rs[s]])

                    def a_sqrt(t):
                        s = t % 2
                        P.op("act", lambda e: e.activation(out=rs[s][:], in_=rs[s][:], func=ACTF.Sqrt), reads=[B_rs[s]], writes=[B_rs[s]])

                    def a_rec(t):
                        s = t % 2
                        P.op("dve", lambda e: e.reciprocal(rs[s][:], rs[s][:]), reads=[B_rs[s]], writes=[B_rs[s]])

                    def a_comb(t):
                        s = t % 2
                        P.op("pool", lambda e: e.tensor_tensor(out=t1[s][:], in0=t1[s][:], in1=t2[s][:], op=ALU.add),
                             reads=[B_t1[s], B_t2[s]], writes=[B_t1[s]])
                        P.op("pool", lambda e: e.tensor_tensor(out=qkr[s][:, 0:512].rearrange("p (h d) -> p h d", d=64),
                                                               in0=t1[s][:, 0:512].rearrange("p (h d) -> p h d", d=64),
                                                               in1=rs[s][:, 0:8, None].broadcast_to([128, 8, 64]), op=ALU.mult),
                             reads=[B_t1[s], B_rs[s]], writes=[B_qkr[s]])
                        P.op("pool", lambda e: e.tensor_tensor(
                            out=kz_dst(qkr[s]),
                            in0=t1[s][:, 512:640].rearrange("p (k d) -> p k d", k=2)[:, :, None, :].broadcast_to([128, 2, 2, 64]),
                            in1=rs[s][:, 8:10, None, None].broadcast_to([128, 2, 2, 64]), op=ALU.mult),
                             reads=[B_t1[s], B_rs[s], B_qkz[s]], writes=[B_qkr[s]])

                    def a_tr(t):
                        s = t % 2
                        P.op("pe", [lambda e, i=i: e.transpose(tpq[s][:, i, :], qkr[s][:, i * 128:(i + 1) * 128], ident[:]) for i in range(8)],
                             reads=[B_qkr[s], B_ident], writes=[B_tpq[s]])

                    def a_ev(t):
                        s = t % 2
                        P.op("act", lambda e: e.copy(qkT_a[:, :, t * 128:(t + 1) * 128], tpq[s][:]), reads=[B_tpq[s]], writes=[B_qkTa[t]])

                    def gate_grp(g):
                        return divmod(g, 4)

                    def a_gate_mm(t):
                        if t % 2 == 0:
                            return
                        for g in (t - 1, t):
                            fb, tc = gate_grp(g)
                            s = g % 2
                            P.op("pe", [lambda e, s=s, kc=kc, fb=fb, tc=tc: e.matmul(gps[s][:], lhsT=W_a[:, kc, 768 + fb * 128:768 + (fb + 1) * 128],
                                                                                        rhs=xT[:, kc, tc * 512:(tc + 1) * 512], start=(kc == 0), stop=(kc == 7))
                                        for kc in range(8)],
                                 reads=[B_Wa[2]] + B_xT[tc * 4:(tc + 1) * 4], writes=[B_gps[s]])

                    def a_gate_act(t):
                        if t % 2 == 0:
                            return
                        for g in (t - 1, t):
                            fb, tc = gate_grp(g)
                            s = g % 2
                            P.op("act", lambda e, s=s, fb=fb, tc=tc: e.activation(out=gT_a[:, fb, tc * 512:(tc + 1) * 512], in_=gps[s][:], func=ACTF.Silu),
                                 reads=[B_gps[s]], writes=[B_gTa[fb][tc]])

                    pipeline([(a_mm, 0), (a_sqrt, 2), (a_sq, 0), (a_rope, 1), (a_red, 1), (a_rec, 2), (a_comb, 2), (a_tr, 3), (a_ev, 3),
                              (a_gate_act, 1), (a_gate_mm, 0)], NT)
                    P.flush()

            with ExitStack() as L3a:
                NS = 3
                Sps = [ps(L3a, "Sps%d" % i, [128, 2, 512], F32) for i in range(NS)]
                acc = ps(L3a, "acc", [128, 2, 512], F32)
                NPT = 3
                Pt = [sb(L3a, "Pt%d" % i, [128, 2, 512], BF16) for i in range(NPT)]
                accs = [sb(L3a, "accs%d" % i, [128, 2, 512], F32) for i in range(2)]
                rec = [sb(L3a, "rec%d" % i, [128, 512], F32) for i in range(2)]
                onm = [sb(L3a, "onm%d" % i, [128, 512], F32) for i in range(2)]
                B_S = [Buf() for _ in range(NS)]
                B_acc = Buf()
                B_accs, B_rec, B_onm = [Buf(), Buf()], [Buf(), Buf()], [Buf(), Buf()]
                B_Pt = [Buf() for _ in range(NPT)]

                d_wb = [dsem("d_wb%d" % i) for i in range(4)]
                for pi in range(4):
                    P.dma_group("pool", [lambda e, kc=kc, pi=pi: e.dma_start(out=W_b[:, kc, pi * 512:(pi + 1) * 512],
                                                                              in_=win_v[:, kc, 1280 + pi * 512:1280 + (pi + 1) * 512])
                                         for kc in range(8)], B_Wb[pi], d_wb[pi])

                steps = [(p, c, kt) for p in range(4) for c in range(4) for kt in range(NT)]

                def emit_S(i):
                    p, c, kt = steps[i]
                    kv = p // 2
                    sl = i % NS
                    keys = slice(kt * 128, (kt + 1) * 128)
                    qs = slice(c * 512, (c + 1) * 512)
                    P.op("pe", [lambda e: e.matmul(Sps[sl][:, 0, :], lhsT=qkT_a[:, 4 + 2 * kv, keys], rhs=qkT_a[:, p, qs], start=True, stop=True),
                                lambda e: e.matmul(Sps[sl][:, 1, :], lhsT=qkT_a[:, 5 + 2 * kv, keys], rhs=qkT_a[:, p, qs], start=True, stop=True)],
                         reads=[B_qkTa[kt]] + B_qkTa[c * 4:(c + 1) * 4], writes=[B_S[sl]])

                def emit_exp(i):
                    sl = i % NS
                    pl = i % NPT
                    P.op("act", lambda e: e.activation(out=Pt[pl][:], in_=Sps[sl][:], func=ACTF.Exp), reads=[B_S[sl]], writes=[B_Pt[pl]])

                def emit_PV(i):
                    p, c, kt = steps[i]
                    kv = p // 2
                    pl = i % NPT
                    al = (i // NT) % 2
                    P.op("pe", [lambda e: e.matmul(acc[:, 0, :], lhsT=v_a[:, kt, kv, :], rhs=Pt[pl][:, 0, :], start=(kt == 0), stop=(kt == NT - 1)),
                                lambda e: e.matmul(acc[:, 1, :], lhsT=v_a2[:, kt, kv, :], rhs=Pt[pl][:, 1, :], start=(kt == 0), stop=(kt == NT - 1))],
                         reads=[B_va[kt], B_vaones, B_Pt[pl]], writes=[B_acc])
                    if kt == NT - 1:
                        qs = slice(c * 512, (c + 1) * 512)
                        P.op("dve", lambda e: e.tensor_copy(accs[al][:], acc[:]), reads=[B_acc], writes=[B_accs[al]])
                        P.op("dve", lambda e: e.reciprocal(rec[al][0:64, :], accs[al][64:128, 0, :]), reads=[B_accs[al]], writes=[B_rec[al]])
                        P.op("dve", lambda e: e.reciprocal(rec[al][64:128, :], accs[al][0:64, 1, :]), reads=[B_accs[al]], writes=[B_rec[al]])
                        P.op("dve", lambda e: e.tensor_tensor(out=onm[al][0:64, :], in0=accs[al][0:64, 0, :], in1=rec[al][0:64, :], op=ALU.mult),
                             reads=[B_accs[al], B_rec[al]], writes=[B_onm[al]])
                        P.op("dve", lambda e: e.tensor_tensor(out=onm[al][64:128, :], in0=accs[al][64:128, 1, :], in1=rec[al][64:128, :], op=ALU.mult),
                             reads=[B_accs[al], B_rec[al]], writes=[B_onm[al]])
                        P.op("pool", lambda e: e.tensor_tensor(out=gT_a[:, p, qs], in0=onm[al][:], in1=gT_a[:, p, qs], op=ALU.mult),
                             reads=[B_onm[al], B_gTa[p][c]], writes=[B_gTa[p][c]])

                n = len(steps)
                emit_S(0)
                emit_S(1)
                for i in range(n):
                    if i + 2 < n:
                        emit_S(i + 2)
                    emit_exp(i)
                    emit_PV(i)
                P.fence(engines=("act", "dve", "pool", "sp"))

        with ExitStack() as LB:
            qT_b = sb(LB, "qT_b", [128, 4, S_LEN], BF16)
            kz_b = sb(LB, "kz_b", [128, 8, S_LEN], BF16)
            B_kz0 = Buf()
            kz4 = kz_b[:].rearrange("p (a b) s -> p a b s", b=2)
            P.op("pool", lambda e: e.memset(kz4[64:128, :, 0, :], 0.0), writes=[B_kz0])
            P.op("pool", lambda e: e.memset(kz4[0:64, :, 1, :], 0.0), writes=[B_kz0])
            v_b = sb(LB, "v_b", [128, NT, 8, 128], BF16)
            nam = sb(LB, "na_msk", [128, 64], F32)
            B_qkTb = [[Buf() for _ in range(4)] for _ in range(8)]
            B_vb = [Buf() for _ in range(NT)]
            B_vbones, B_nam = Buf(), Buf()
            d_nam = dsem("d_nam")
            P.op("pool", lambda e: e.memset(v_b[:, :, :, 64:128], 1.0), writes=[B_vbones])
            P.op("sp", lambda e: e.dma_start(out=nam[:], in_=nam_d[:, :]), writes=[B_nam], dma=d_nam)
            tabh = [sb(LB, "tabh%d" % i, [128, 15, 64], F32) for i in range(2)]
            tabb = [sb(LB, "tabb%d" % i, [128, 16, 64], BF16) for i in range(2)]
            tabm = [sb(LB, "tabm%d" % i, [128, 16, 64], BF16) for i in range(2)]
            B_tabh, B_tabb, B_tabm = [Buf(), Buf()], [Buf(), Buf()], [Buf(), Buf()]
            d_tab = [dsem("d_tab0"), dsem("d_tab1")]

            def load_tab(h):
                ts = h % 2
                P.op("sp", lambda e: e.dma_start(out=tabh[ts][:].rearrange("p b c -> p (b c)"), in_=nab_d[:, h * 960:(h + 1) * 960]),
                     writes=[B_tabh[ts]], dma=d_tab[ts])
                P.op("pool", lambda e: e.tensor_tensor(out=tabb[ts][:, 0:15, :], in0=tabh[ts][:], in1=nam[:, None, :].broadcast_to([128, 15, 64]), op=ALU.add),
                     reads=[B_tabh[ts], B_nam], writes=[B_tabb[ts]])
                P.op("pool", lambda e: e.tensor_copy(tabm[ts][:, 0:15, :], tabb[ts][:, 0:15, :]), reads=[B_tabb[ts]], writes=[B_tabm[ts]])
                P.op("pool", lambda e: e.memset(tabm[ts][64:128, 4, :], NEG), writes=[B_tabm[ts]])
                P.op("pool", lambda e: e.memset(tabm[ts][0:64, 12, :], NEG), writes=[B_tabm[ts]])

            for ts_ in range(2):
                P.op("pool", lambda e, ts_=ts_: e.memset(tabb[ts_][:, 15:16, :], 0.0), writes=[B_tabb[ts_]])
                P.op("pool", lambda e, ts_=ts_: e.memset(tabm[ts_][:, 15:16, :], 0.0), writes=[B_tabm[ts_]])
            load_tab(0)

            with ExitStack() as L2b:
                NFP = 6
                fps = [ps(L2b, "fps%d" % i, [128, 512], F32) for i in range(NFP)]
                vps = [ps(L2b, "vps%d" % i, [128, 512], F32) for i in range(2)]
                B_fps = [Buf() for _ in range(NFP)]
                B_vps = [Buf(), Buf()]
                fi = 0
                for grp in (0, 1, 3):
                    for fb in range(4):
                        for tc in range(4):
                            s = fi % NFP
                            fi += 1
                            col0 = grp * 512 + fb * 128
                            P.op("pe", [lambda e, s=s, kc=kc, col0=col0, tc=tc: e.matmul(fps[s][:], lhsT=W_b[:, kc, col0:col0 + 128],
                                                                                          rhs=xT[:, kc, tc * 512:(tc + 1) * 512], start=(kc == 0), stop=(kc == 7))
                                        for kc in range(8)],
                                 reads=[B_Wb[grp]] + B_xT[tc * 4:(tc + 1) * 4], writes=[B_fps[s]])
                            qs = slice(tc * 512, (tc + 1) * 512)
                            if grp == 0:
                                P.op("dve", lambda e, s=s, fb=fb, qs=qs: e.tensor_scalar(qT_b[:, fb, qs], fps[s][:], 0.125, None, ALU.mult),
                                     reads=[B_fps[s]], writes=[B_qkTb[fb][tc]])
                            elif grp == 1:
                                P.op("act", lambda e, s=s, fb=fb, qs=qs: e.copy(kz_b[0:64, 2 * fb, qs], fps[s][0:64, :]), reads=[B_fps[s]], writes=[B_qkTb[4 + fb][tc]])
                                P.op("act", lambda e, s=s, fb=fb, qs=qs: e.copy(kz_b[64:128, 2 * fb + 1, qs], fps[s][64:128, :]), reads=[B_fps[s]], writes=[B_qkTb[4 + fb][tc]])
                            else:
                                P.op("act", lambda e, s=s, fb=fb, qs=qs: e.activation(out=gT_b[:, fb, qs], in_=fps[s][:], func=ACTF.Silu),
                                     reads=[B_fps[s]], writes=[B_gTb[fb][tc]])
                    if grp == 1:
                        for t in range(NT):
                            s = t % 2
                            tok = slice(t * 128, (t + 1) * 128)
                            P.op("pe", [lambda e, s=s, kc=kc, tok=tok: e.matmul(vps[s][:], lhsT=xT[:, kc, tok], rhs=W_b[:, kc, 1024:1536],
                                                                                 start=(kc == 0), stop=(kc == 7)) for kc in range(8)],
                                 reads=[B_xT[t], B_Wb[2]], writes=[B_vps[s]])
                            P.op("dve", lambda e, s=s, t=t: e.tensor_copy(v_b[:, t, :, 0:64], vps[s][:].rearrange("p (h d) -> p h d", d=64)),
                                 reads=[B_vps[s]], writes=[B_vb[t]])
                P.flush()

            with ExitStack() as L34:
                w_o = W_b[:, 0:4, :].rearrange("p a (b c) -> p (a b) c", c=D)
                fg_bc = W_b[:, 4, :].bitcast(F32)
                B_wo, B_fg = Buf(), Buf()
                d_wo = dsem("d_wo")
                d_fg = dsem("d_fg")
                P.dma_group("pool", [lambda e, kc=kc: e.dma_start(out=w_o[:, kc, :], in_=wout_v[:, kc, :]) for kc in range(8)], B_wo, d_wo)
                P.op("sp", lambda e: e.dma_start(out=fg_bc, in_=fg_d.partition_broadcast(128)), writes=[B_fg], dma=d_fg)

                with ExitStack() as L3b:
                    accn = ps(L3b, "accn", [128, 4, 512], F32)
                    Sn = [ps(L3b, "Sn%d" % i, [128, 2, 512], F32) for i in range(2)]
                    NPB = 3
                    accs = xT[:, 0:2, :].bitcast(F32).rearrange("p a (b c) -> p (a b) c", c=512)
                    Os = xT[:, 0, :].bitcast(F32)
                    recn = xT[:, 2, :].bitcast(F32)
                    acc2 = xT[:, 3, :].bitcast(F32)
                    onn2 = [xT[:, 4, 0:1024].bitcast(F32), xT[:, 4, 1024:2048].bitcast(F32)]
                    B_onn2 = [Buf(), Buf()]
                    Pn = [xT[:, 5, 0:1024], xT[:, 5, 1024:2048], xT[:, 6, 0:1024]]
                    B_accn, B_accs, B_recn, B_onn, B_acc2 = Buf(), Buf(), Buf(), Buf(), Buf()
                    B_Sn = [Buf(), Buf()]
                    B_Pn = [Buf() for _ in range(NPB)]

                    def na_geom(j):
                        kr = (2 * j, 2 * j + 1)
                        rows = [r for r in range(ROWS) if any(_r0(r) <= k < _r0(r) + 8 for k in kr)]
                        ra, rb = rows[0], rows[-1] + 1
                        assert rows == list(range(ra, rb)) and rb - ra <= 16
                        n1 = (rb - ra + 1) // 2
                        chunks = [(ra, ra + n1), (ra + n1, rb)]
                        return kr, ra, rb, n1, chunks

                    def na_masks(j):
                        kr, ra, rb, n1, chunks = na_geom(j)
                        use_m, fix = [], []
                        for (ca, cb) in chunks:
                            true_v = {r: tuple(_r0(r) <= k < _r0(r) + 8 for k in kr) for r in range(ca, cb)}
                            rule_v = {}
                            for r in range(ca, cb):
                                b = r - 2 * j + 7
                                rule_v[r] = (True, False) if b == 4 else ((False, True) if b == 12 else (True, True))
                            if rule_v == true_v:
                                use_m.append(any(v != (True, True) for v in true_v.values()))
                            else:
                                use_m.append(False)
                                fix += [(r, v) for r, v in true_v.items() if v != (True, True)]
                        return use_m, fix

                    steps = [(h, j) for h in range(8) for j in range(NT)]
                    nst = len(steps)
                    B_acch = [Buf(), Buf()]
                    B_Os, B_rec2 = [Buf(), Buf()], [Buf(), Buf()]
                    J_FIRST1 = min(j_ for j_ in range(NT) if na_geom(j_)[2] > 16)
                    zw = sb(L3b, "zw", [128, 128], BF16)
                    B_zw = Buf()
                    P.op("pool", lambda e: e.memset(zw[:], 0.0), writes=[B_zw])

                    def na_S(i):
                        h, j = steps[i]
                        p = h // 2
                        kr, ra, rb, n1, chunks = na_geom(j)
                        sl = i % 2
                        ts = h % 2
                        keys = slice(j * 128, (j + 1) * 128)
                        b0 = ra - 2 * j + 7
                        assert 0 <= b0 and b0 + 2 * n1 <= 16
                        use_m, _fix = na_masks(j)
                        if j == 0 and h + 1 < 8:
                            load_tab(h + 1)
                        mm = []
                        for ci, (ca, cb) in enumerate(chunks):
                            mm.append(lambda e, ci=ci, ca=ca, cb=cb: e.matmul(
                                Sn[sl][:, ci, 0:(cb - ca) * 64], lhsT=kz_b[:, h, keys], rhs=qT_b[:, p, ca * 64:cb * 64], start=True, stop=True))
                        for ci, (ca, cb) in enumerate(chunks):
                            bb = b0 + ci * n1
                            tb = tabm[ts] if use_m[ci] else tabb[ts]
                            mm.append(lambda e, ci=ci, bb=bb, tb=tb: e.matmul(
                                Sn[sl][:, ci, 0:n1 * 64], lhsT=ident[:], rhs=tb[:, bb:bb + n1, :].rearrange("p r c -> p (r c)"),
                                start=False, stop=True, skip_group_check=True))
                        P.op("pe", mm,
                             reads=[B_qkTb[4 + p][j // 4], B_kz0, B_ident, B_tabb[ts], B_tabm[ts]] + [B_qkTb[p][tc] for tc in range((ra * 64) // 512, ((rb * 64 - 1) // 512) + 1)],
                             writes=[B_Sn[sl]])

                    def na_rest(i):
                        h, j = steps[i]
                        p, hh = h // 2, h % 2
                        ln = slice(hh * 64, hh * 64 + 64)
                        kr, ra, rb, n1, chunks = na_geom(j)
                        sl = i % 2
                        pl = i % NPB
                        w1 = n1 * 64
                        P.op("act", lambda e: e.activation(out=Pn[pl][:, 0:2 * w1].rearrange("p (a w) -> p a w", a=2), in_=Sn[sl][:, :, 0:w1], func=ACTF.Exp),
                             reads=[B_Sn[sl]], writes=[B_Pn[pl]])
                        for r, val in na_masks(j)[1]:
                            part = slice(64, 128) if val == (True, False) else slice(0, 64)
                            P.op("pool", lambda e, part=part, r=r: e.memset(Pn[pl][part, (r - ra) * 64:(r - ra + 1) * 64], 0.0),
                                 reads=[B_Pn[pl]], writes=[B_Pn[pl]])

                    def na_pv(i):
                        h, j = steps[i]
                        p, hh = h // 2, h % 2
                        ln = slice(hh * 64, hh * 64 + 64)
                        kr, ra, rb, n1, chunks = na_geom(j)
                        pl = i % NPB
                        mm = []
                        rearm = ([0] if j == 0 else []) + ([1] if j == J_FIRST1 else [])
                        for hf_ in rearm:
                            for qb_ in (2 * hf_, 2 * hf_ + 1):
                                mm.append(lambda e, qb_=qb_: e.matmul(accn[:, qb_, :], lhsT=zw[:], rhs=qT_b[:, 0, 0:512], start=True, stop=True))
                        r = ra
                        while r < rb:
                            qb = r // 8
                            r_e = min(rb, (qb + 1) * 8)
                            mm.append(lambda e, qb=qb, r=r, r_e=r_e: e.matmul(
                                accn[:, qb, (r - 8 * qb) * 64:(r_e - 8 * qb) * 64], lhsT=v_b[:, j, h, :],
                                rhs=Pn[pl][:, (r - ra) * 64:(r_e - ra) * 64], start=False, stop=True, skip_group_check=True))
                            r = r_e
                        halves = sorted(set(rearm) | {(r_ // 8) // 2 for r_ in range(ra, rb)})
                        P.op("pe", mm, reads=[B_vb[j], B_vbones, B_Pn[pl]] + ([B_zw, B_qkTb[0][0]] if rearm else []),
                             writes=[B_acch[hf_] for hf_ in halves])
                        if j == NT - 1:
                            for hf_ in range(2):
                                lo = slice(64 * hf_, 64 * hf_ + 64)
                                q0 = 2 * hf_
                                P.op("dve", lambda e, lo=lo, q0=q0: e.tensor_copy(Os[lo, :], accn[0:64, q0:q0 + 2, :].rearrange("p a b -> p (a b)")),
                                     reads=[B_acch[hf_]], writes=[B_Os[hf_]])
                                P.op("dve", lambda e, lo=lo, q0=q0: e.tensor_copy(recn[lo, :], accn[64:128, q0:q0 + 2, :].rearrange("p a b -> p (a b)")),
                                     reads=[B_acch[hf_]], writes=[B_rec2[hf_]])
                            P.op("dve", lambda e: e.reciprocal(recn, recn), reads=[B_rec2[0], B_rec2[1]], writes=[B_rec2[0], B_rec2[1]])
                            for qb in range(4):
                                qs = slice(qb * 512, (qb + 1) * 512)
                                rc = slice((qb % 2) * 512, (qb % 2) * 512 + 512)
                                hf_ = qb // 2
                                lo = slice(64 * hf_, 64 * hf_ + 64)
                                on_ = onn2[qb % 2]
                                Bon = B_onn2[qb % 2]
                                P.op("dve", lambda e, rc=rc, on_=on_, lo=lo: e.tensor_tensor(out=on_[ln, :], in0=Os[lo, rc], in1=recn[lo, rc], op=ALU.mult),
                                     reads=[B_Os[hf_], B_rec2[hf_]], writes=[Bon])
                                P.op("pool", lambda e, qs=qs, on_=on_: e.tensor_tensor(out=gT_b[ln, p, qs], in0=on_[ln, :], in1=gT_b[ln, p, qs], op=ALU.mult),
                                     reads=[Bon, B_gTb[p][qb]], writes=[B_gTb[p][qb]])

                    na_S(0)
                    na_S(1)
                    for i in range(nst):
                        na_rest(i)
                        if i + 2 < nst:
                            na_S(i + 2)
                        na_pv(i)
                    P.flush()

                with ExitStack() as L4:
                    yps = [ps(L4, "yps%d" % i, [128, 2, 512], F32) for i in range(2)]
                    xr = [xT[:, k, :].bitcast(F32) for k in range(3)]
                    yt = [xT[:, 3 + k, :].bitcast(F32) for k in range(2)]
                    yo = [xT[:, 5 + k, :].bitcast(F32) for k in range(2)]
                    sq4 = [xT[:, 7, 0:1024], xT[:, 7, 1024:2048]]
                    st4 = sb(L4, "st4", [128, NT, 4], F32)
                    B_yps, B_xr, B_yt, B_yo, B_sq4 = ([Buf(), Buf(), Buf()] for _ in range(5))
                    B_st4 = [Buf() for _ in range(NT)]
                    B_st4all = Buf()
                    d_xr = [dsem("d_xr0"), dsem("d_xr1"), dsem("d_xr2")]
                    d_yo = [dsem("d_yo0"), dsem("d_yo1")]
                    P.op("dve", lambda e: e.memset(st4[:], 0.0), writes=[B_st4all])
                    def o_ld(t):
                        s3 = t % 3
                        P.op("sp", lambda e: e.dma_start(out=xr[s3][:], in_=x_d[t * 128:(t + 1) * 128, :]), writes=[B_xr[s3]], dma=d_xr[s3])

                    def o_mm(t):
                        s = t % 2
                        tok = slice(t * 128, (t + 1) * 128)
                        mm = []
                        for nh in range(2):
                            for fc in range(8):
                                src = gT_a if fc < 4 else gT_b
                                mm.append(lambda e, nh=nh, fc=fc, src=src: e.matmul(
                                    yps[s][:, nh, :], lhsT=src[:, fc % 4, tok], rhs=w_o[:, fc, nh * 512:(nh + 1) * 512], start=(fc == 0), stop=(fc == 7)))
                        P.op("pe", mm, reads=[B_wo] + [B_gTa[fb][t // 4] for fb in range(4)] + [B_gTb[fb][t // 4] for fb in range(4)], writes=[B_yps[s]])

                    def o_add(t):
                        s, s3 = t % 2, t % 3
                        P.op("dve", lambda e: e.tensor_tensor(out=yt[s][:], in0=yps[s][:].rearrange("p a b -> p (a b)"), in1=xr[s3][:], op=ALU.add),
                             reads=[B_yps[s], B_xr[s3]], writes=[B_yt[s]])

                    def o_sq(t):
                        s = t % 2
                        P.op("act", lambda e: e.activation(out=sq4[0][:], in_=yt[s][:], func=ACTF.Square, accum_out=st4[:, t, 0:1]),
                             reads=[B_yt[s], B_st4all], writes=[B_sq4[0], B_st4[t]])

                    def o_scale(t):
                        P.op("dve", lambda e: e.tensor_scalar(st4[:, t, 1:2], st4[:, t, 0:1], 1.0 / D, EPS, ALU.mult, ALU.add),
                             reads=[B_st4[t]], writes=[B_st4[t]])

                    def o_sqrt(t):
                        P.op("act", lambda e: e.activation(out=st4[:, t, 1:2], in_=st4[:, t, 1:2], func=ACTF.Sqrt), reads=[B_st4[t]], writes=[B_st4[t]])

                    def o_fin(t):
                        s = t % 2
                        tok = slice(t * 128, (t + 1) * 128)
                        P.op("dve", lambda e: e.reciprocal(st4[:, t, 2:3], st4[:, t, 1:2]), reads=[B_st4[t]], writes=[B_st4[t]])
                        P.op("dve", lambda e: e.scalar_tensor_tensor(out=yo[s][:], in0=yt[s][:], scalar=st4[:, t, 2:3], in1=fg_bc,
                                                                      op0=ALU.mult, op1=ALU.mult),
                             reads=[B_yt[s], B_st4[t], B_fg], writes=[B_yo[s]])
                        P.op("sp", lambda e: e.dma_start(out=y_d[tok, :], in_=yo[s][:]), reads=[B_yo[s]], dma=d_yo[s])

                    o_ld(0)
                    pipeline([(lambda t: o_ld(t + 1) if t + 1 < NT else None, 0), (o_mm, 0), (o_scale, 1), (o_sqrt, 1), (o_add, 0), (o_fin, 1), (o_sq, 0)], NT)
                    for d in d_yo:
                        P.wait("sp", (d.sem, d.count))
                    P.flush()
    return nc


def _rope_tables():
    t = np.arange(S_LEN)
    row = (t // GRID_W).astype(np.float32)
    col = (t % GRID_W).astype(np.float32)
    inv = (np.float32(10000.0) ** (-np.arange(16, dtype=np.float32) * np.float32(2.0 / 32))).astype(np.float32)
    ang_r = row[:, None] * inv[None, :]
    ang_c = col[:, None] * inv[None, :]
    ang = np.concatenate([ang_r, ang_r, ang_c, ang_c], axis=-1).astype(np.float32)
    cos = np.cos(ang).astype(np.float32)
    sin = np.sin(ang).astype(np.float32)
    sgn = np.ones(64, np.float32)
    sgn[0:16] = -1.0
    sgn[32:48] = -1.0
    ssin = sin * sgn[None, :]

    def lay(a):
        return np.ascontiguousarray(a.reshape(NT, 128, 64).transpose(1, 0, 2).reshape(128, NT * 64))
    return lay(cos), lay(ssin)


def _na_layout(rpb):
    krl = np.arange(128) // 64
    kc = np.arange(128) % 64
    b = np.arange(15)
    c = np.arange(64)
    a = np.clip(14 - b[None, :] + krl[:, None], 0, 14)
    dc = np.clip(kc[:, None] - c[None, :] + 15, 0, 30)
    g = rpb[:, a[:, :, None], dc[:, None, :]]
    g = np.ascontiguousarray(g.transpose(1, 0, 2, 3)).reshape(128, 8 * 15 * 64).astype(np.float32)
    cs = np.clip(c - 8, 0, GRID_W - 16)
    valid = (kc[:, None] >= cs[None, :]) & (kc[:, None] < cs[None, :] + 16)
    mask = np.where(valid, 0.0, NEG).astype(np.float32)
    return g, np.ascontiguousarray(mask)


_NC_CACHE = {}


def kernel(x, norm_gain, w_in, q_norm_a, k_norm_a, na_rpb, w_out, final_norm_gain):
    x = np.asarray(x, np.float32)
    n = x.shape[0]
    if "nc" not in _NC_CACHE:
        _NC_CACHE["nc"] = build_nc()
    nc = _NC_CACHE["nc"]
    cos_t, sin_t = _rope_tables()
    nab, nam = _na_layout(np.asarray(na_rpb, np.float32)[0])
    common = {
        "w_in": np.ascontiguousarray(np.asarray(w_in, np.float32)[0]),
        "w_out": np.ascontiguousarray(np.asarray(w_out, np.float32)[0]),
        "norm_gain": np.ascontiguousarray(np.asarray(norm_gain, np.float32)[0]),
        "final_norm_gain": np.ascontiguousarray(np.asarray(final_norm_gain, np.float32)),
        "q_norm_a": np.ascontiguousarray(np.asarray(q_norm_a, np.float32)[0]),
        "k_norm_a": np.ascontiguousarray(np.asarray(k_norm_a, np.float32)[0]),
        "cos_tab": cos_t, "sin_tab": sin_t, "na_bias": nab, "na_mask": nam,
    }
    in_maps = [dict(common, x=np.ascontiguousarray(x[b])) for b in range(n)]
    res = run_bass_kernel_spmd(nc, in_maps, core_ids=list(range(n)))
    return np.stack([np.asarray(r["y"], np.float32) for r in res.results], axis=0)
```
